# Optimizing a Trainium2 kernel written in Bass

```python
import math
import jax, jax.numpy as jnp
from jax import lax
import numpy as np

D_MODEL = 1024
BATCH = 8
SEQ = 2048
DEPTH = 1
DEC_BATCH = 8
DEC_SEQ = 8192
PAST_LEN = 128

SSD_WIDTH = D_MODEL
SSD_HEADDIM = 64
SSD_HEADS = SSD_WIDTH // SSD_HEADDIM
SSD_GROUPS = 2
SSD_HPG = SSD_HEADS // SSD_GROUPS
SSD_STATE = 128
SSD_CHUNK = 128
CONV_K = 5
CONV_CH = SSD_WIDTH + 2 * SSD_GROUPS * SSD_STATE
GLA_HEADS = 4
GLA_KEY = D_MODEL // 2
GLA_VAL = D_MODEL
GLA_DK = GLA_KEY // GLA_HEADS
GLA_DV = GLA_VAL // GLA_HEADS
GLA_LOWRANK = 16
GLA_GATE_NORM = 16.0
GLA_CHUNK = 64
MIX_WIDTH = SSD_WIDTH + GLA_VAL
IN_SIZES = (SSD_WIDTH, SSD_WIDTH, SSD_GROUPS * SSD_STATE, SSD_GROUPS * SSD_STATE, SSD_HEADS, SSD_HEADS,
            GLA_KEY, GLA_KEY, GLA_VAL, GLA_VAL, GLA_LOWRANK)
IN_PROJ = sum(IN_SIZES)
MEM_TOKENS = 256
MEM_HEADS = 4
MEM_HD = D_MODEL // MEM_HEADS
D_FF = 4 * D_MODEL
ALPHA = (2 * DEPTH) ** 0.25
BETA = (8 * DEPTH) ** -0.25
LN_EPS = 1e-5
RMS_EPS = 1e-5

kernel_name = "hybrid_ssd_gla_memory_encoder"


def split_cols(a, sizes):
    offs = [0]
    for s in sizes:
        offs.append(offs[-1] + s)
    return [a[..., offs[i]:offs[i + 1]] for i in range(len(sizes))]


def layer_norm(x, g, b):
    xf = x.astype(jnp.float32)
    mu = jnp.mean(xf, -1, keepdims=True)
    var = jnp.mean(jnp.square(xf - mu), -1, keepdims=True)
    return ((xf - mu) * lax.rsqrt(var + LN_EPS) * g + b).astype(x.dtype)


def rms_norm(x, g):
    xf = x.astype(jnp.float32)
    return xf * lax.rsqrt(jnp.mean(jnp.square(xf), -1, keepdims=True) + RMS_EPS) * g


def centred_dwconv(u, w, b):
    out = lax.conv_general_dilated(u, w[:, None, :], window_strides=(1,),
                                   padding=[((CONV_K - 1) // 2, CONV_K // 2)],
                                   dimension_numbers=('NWC', 'WIO', 'NWC'),
                                   feature_group_count=u.shape[-1])
    return out + b


def ssd_chunked(x, dt, a_neg, bm, cm):
    b, L = x.shape[:2]
    Q = SSD_CHUNK
    nc = L // Q
    xc = x.reshape(b, nc, Q, SSD_GROUPS, SSD_HPG, SSD_HEADDIM)
    dtc = dt.reshape(b, nc, Q, SSD_GROUPS, SSD_HPG)
    bc = bm.reshape(b, nc, Q, SSD_GROUPS, SSD_STATE)
    cc = cm.reshape(b, nc, Q, SSD_GROUPS, SSD_STATE)
    acs = jnp.cumsum(dtc * a_neg.reshape(SSD_GROUPS, SSD_HPG), axis=2)
    causal = jnp.tril(jnp.ones((Q, Q), bool))[:, :, None, None]
    seg = acs[:, :, :, None] - acs[:, :, None, :]
    lmat = jnp.exp(jnp.where(causal, seg, -jnp.inf))
    xdt = xc * dtc[..., None]
    cb = jnp.einsum('bclgn,bcsgn->bclsg', cc, bc)
    y_diag = jnp.einsum('bclsg,bclsgh,bcsghp->bclghp', cb, lmat, xdt)
    decay_to_end = jnp.exp(acs[:, :, -1:] - acs)
    chunk_states = jnp.einsum('bcsgn,bcsgh,bcsghp->bcghpn', bc, decay_to_end, xdt)
    chunk_decay = jnp.exp(acs[:, :, -1])

    def step(h, inp):
        st, dec = inp
        return h * dec[..., None, None] + st, h

    h0 = jnp.zeros((b, SSD_GROUPS, SSD_HPG, SSD_HEADDIM, SSD_STATE), jnp.float32)
    _, h_in = lax.scan(step, h0, (jnp.moveaxis(chunk_states, 1, 0), jnp.moveaxis(chunk_decay, 1, 0)))
    h_in = jnp.moveaxis(h_in, 0, 1)
    y_off = jnp.einsum('bclgn,bcghpn,bclgh->bclghp', cc, h_in, jnp.exp(acs))
    return (y_diag + y_off).reshape(b, L, SSD_HEADS, SSD_HEADDIM)


def gla_chunked(q, k, v, lg):
    b, L, H, K = q.shape
    V = v.shape[-1]
    Q = GLA_CHUNK
    nc = L // Q
    qc = q.reshape(b, nc, Q, H, K)
    kc = k.reshape(b, nc, Q, H, K)
    vc = v.reshape(b, nc, Q, H, V)
    bcs = jnp.cumsum(lg.reshape(b, nc, Q, H, K), axis=2)
    q_t = qc * jnp.exp(bcs)
    k_t = kc * jnp.exp(-bcs)
    incl = jnp.tril(jnp.ones((Q, Q), bool))
    att = jnp.where(incl, jnp.einsum('bclhk,bcshk->bchls', q_t, k_t), 0.0)
    o_intra = jnp.einsum('bchls,bcshv->bclhv', att, vc)
    k_end = kc * jnp.exp(bcs[:, :, -1:] - bcs)
    chunk_states = jnp.einsum('bcshk,bcshv->bchkv', k_end, vc)
    chunk_decay = jnp.exp(bcs[:, :, -1])

    def step(h, inp):
        st, dec = inp
        return h * dec[..., None] + st, h

    h0 = jnp.zeros((b, H, K, V), jnp.float32)
    _, h_in = lax.scan(step, h0, (jnp.moveaxis(chunk_states, 1, 0), jnp.moveaxis(chunk_decay, 1, 0)))
    h_in = jnp.moveaxis(h_in, 0, 1)
    o_inter = jnp.einsum('bclhk,bchkv->bclhv', q_t, h_in)
    return (o_intra + o_inter).reshape(b, L, H, V)


def flip(a):
    return jnp.flip(a, axis=1)


def hybrid_mixer(x, w_in, conv_w, conv_b, a_log_f, a_log_b, dt_bias_f, dt_bias_b, d_skip, ssd_norm_g,
                 w_gk_f, b_gk_f, w_gk_b, b_gk_b, gla_norm_g, w_out):
    b, L, _ = x.shape
    f32 = jnp.float32
    z, xs, bm, cm, dt_f, dt_b, q, k, v, g, gk_lr = split_cols(x @ w_in, IN_SIZES)
    xbc = jax.nn.silu(centred_dwconv(jnp.concatenate([xs, bm, cm], -1), conv_w, conv_b))
    xs, bm, cm = split_cols(xbc, (SSD_WIDTH, SSD_GROUPS * SSD_STATE, SSD_GROUPS * SSD_STATE))
    xh = xs.reshape(b, L, SSD_HEADS, SSD_HEADDIM).astype(f32)
    bm = bm.reshape(b, L, SSD_GROUPS, SSD_STATE).astype(f32)
    cm = cm.reshape(b, L, SSD_GROUPS, SSD_STATE).astype(f32)
    dtf = jax.nn.softplus(dt_f.astype(f32) + dt_bias_f.astype(f32))
    dtb = jax.nn.softplus(dt_b.astype(f32) + dt_bias_b.astype(f32))
    a_f = -jnp.exp(a_log_f.astype(f32))
    a_b = -jnp.exp(a_log_b.astype(f32))
    y = (ssd_chunked(xh, dtf, a_f, bm, cm)
         + flip(ssd_chunked(flip(xh), flip(dtb), a_b, flip(bm), flip(cm)))
         + d_skip.astype(f32)[:, None] * xh)
    y = y.reshape(b, L, SSD_WIDTH) * jax.nn.silu(z.astype(f32))
    y = rms_norm(y.reshape(b, L, SSD_GROUPS, SSD_WIDTH // SSD_GROUPS),
                 ssd_norm_g.reshape(SSD_GROUPS, -1)).reshape(b, L, SSD_WIDTH)
    qh = q.reshape(b, L, GLA_HEADS, GLA_DK).astype(f32) * (GLA_DK ** -0.5)
    kh = k.reshape(b, L, GLA_HEADS, GLA_DK).astype(f32)
    vh = v.reshape(b, L, GLA_HEADS, GLA_DV).astype(f32)
    lr = gk_lr.astype(f32)
    lg_f = (jax.nn.log_sigmoid(lr @ w_gk_f.astype(f32) + b_gk_f) / GLA_GATE_NORM).reshape(b, L, GLA_HEADS, GLA_DK)
    lg_b = (jax.nn.log_sigmoid(lr @ w_gk_b.astype(f32) + b_gk_b) / GLA_GATE_NORM).reshape(b, L, GLA_HEADS, GLA_DK)
    o = gla_chunked(qh, kh, vh, lg_f) + flip(gla_chunked(flip(qh), flip(kh), flip(vh), flip(lg_b)))
    o = rms_norm(o, gla_norm_g) * jax.nn.silu(g.reshape(b, L, GLA_HEADS, GLA_DV).astype(f32))
    mix = jnp.concatenate([y, o.reshape(b, L, GLA_VAL)], -1).astype(x.dtype)
    return mix @ w_out


def memory_cross_attention(x, mem, w_mq, w_mk, w_mv, w_mo):
    b, L, _ = x.shape
    m = mem.shape[1]
    q = (x @ w_mq).reshape(b, L, MEM_HEADS, MEM_HD)
    k = (mem @ w_mk).reshape(b, m, MEM_HEADS, MEM_HD)
    v = (mem @ w_mv).reshape(b, m, MEM_HEADS, MEM_HD)
    s = jnp.einsum('blhd,bmhd->bhlm', q, k).astype(jnp.float32) * (MEM_HD ** -0.5)
    p = jax.nn.softmax(s, axis=-1).astype(v.dtype)
    o = jnp.einsum('bhlm,bmhd->blhd', p, v).reshape(b, L, D_MODEL)
    return o @ w_mo


def squared_relu_mlp(x, w_ff1, w_ff2):
    return jnp.square(jax.nn.relu(x @ w_ff1)) @ w_ff2


def encoder_layer(x, mem, w_in, conv_w, conv_b, a_log_f, a_log_b, dt_bias_f, dt_bias_b, d_skip, ssd_norm_g,
                  w_gk_f, b_gk_f, w_gk_b, b_gk_b, gla_norm_g, w_out, ln1_g, ln1_b,
                  w_mq, w_mk, w_mv, w_mo, ln2_g, ln2_b, w_ff1, w_ff2, ln3_g, ln3_b):
    x = layer_norm(ALPHA * x + hybrid_mixer(x, w_in, conv_w, conv_b, a_log_f, a_log_b, dt_bias_f, dt_bias_b,
                                             d_skip, ssd_norm_g, w_gk_f, b_gk_f, w_gk_b, b_gk_b,
                                             gla_norm_g, w_out), ln1_g, ln1_b)
    x = layer_norm(ALPHA * x + memory_cross_attention(x, mem, w_mq, w_mk, w_mv, w_mo), ln2_g, ln2_b)
    x = layer_norm(ALPHA * x + squared_relu_mlp(x, w_ff1, w_ff2), ln3_g, ln3_b)
    return x


def run_trunk(x, mem, params):
    for i in range(DEPTH):
        x = encoder_layer(x, mem, *[p[i] for p in params])
    return x


def setup_inputs(seed: int = 0) -> dict:
    key = jax.random.key(seed)
    ks = jax.random.split(key, 40)
    f32 = jnp.float32
    nrm = lambda k, shape, scale: jax.random.normal(k, shape, f32) * scale
    dt0 = jnp.exp(jax.random.uniform(ks[8], (DEPTH, SSD_HEADS), f32, math.log(1e-3), math.log(1e-1)))
    dt1 = jnp.exp(jax.random.uniform(ks[9], (DEPTH, SSD_HEADS), f32, math.log(1e-3), math.log(1e-1)))
    inv_softplus = lambda d: d + jnp.log(-jnp.expm1(-d))
    return {
        "x_prompt": nrm(ks[0], (BATCH, SEQ, D_MODEL), 1.0),
        "x_sample": nrm(ks[1], (DEC_BATCH, DEC_SEQ, D_MODEL), 1.0),
        "mem_prompt": nrm(ks[2], (BATCH, MEM_TOKENS, D_MODEL), 1.0),
        "mem_sample": nrm(ks[3], (DEC_BATCH, MEM_TOKENS, D_MODEL), 1.0),
        "w_in": nrm(ks[4], (DEPTH, D_MODEL, IN_PROJ), D_MODEL ** -0.5),
        "conv_w": nrm(ks[5], (DEPTH, CONV_K, CONV_CH), CONV_K ** -0.5),
        "conv_b": nrm(ks[6], (DEPTH, CONV_CH), 0.02),
        "a_log_f": jnp.log(jax.random.uniform(ks[7], (DEPTH, SSD_HEADS), f32, 1.0, 16.0)),
        "a_log_b": jnp.log(jax.random.uniform(ks[10], (DEPTH, SSD_HEADS), f32, 1.0, 16.0)),
        "dt_bias_f": inv_softplus(dt0),
        "dt_bias_b": inv_softplus(dt1),
        "d_skip": 1.0 + nrm(ks[11], (DEPTH, SSD_HEADS), 0.02),
        "ssd_norm_g": 1.0 + nrm(ks[12], (DEPTH, SSD_WIDTH), 0.02),
        "w_gk_f": nrm(ks[13], (DEPTH, GLA_LOWRANK, GLA_KEY), GLA_LOWRANK ** -0.5),
        "b_gk_f": nrm(ks[14], (DEPTH, GLA_KEY), 0.02),
        "w_gk_b": nrm(ks[15], (DEPTH, GLA_LOWRANK, GLA_KEY), GLA_LOWRANK ** -0.5),
        "b_gk_b": nrm(ks[16], (DEPTH, GLA_KEY), 0.02),
        "gla_norm_g": 1.0 + nrm(ks[17], (DEPTH, GLA_DV), 0.02),
        "w_out": nrm(ks[18], (DEPTH, MIX_WIDTH, D_MODEL), MIX_WIDTH ** -0.5 * BETA),
        "ln1_g": 1.0 + nrm(ks[19], (DEPTH, D_MODEL), 0.02),
        "ln1_b": nrm(ks[20], (DEPTH, D_MODEL), 0.02),
        "w_mq": nrm(ks[21], (DEPTH, D_MODEL, D_MODEL), D_MODEL ** -0.5),
        "w_mk": nrm(ks[22], (DEPTH, D_MODEL, D_MODEL), D_MODEL ** -0.5),
        "w_mv": nrm(ks[23], (DEPTH, D_MODEL, D_MODEL), D_MODEL ** -0.5 * BETA),
        "w_mo": nrm(ks[24], (DEPTH, D_MODEL, D_MODEL), D_MODEL ** -0.5 * BETA),
        "ln2_g": 1.0 + nrm(ks[25], (DEPTH, D_MODEL), 0.02),
        "ln2_b": nrm(ks[26], (DEPTH, D_MODEL), 0.02),
        "w_ff1": nrm(ks[27], (DEPTH, D_MODEL, D_FF), D_MODEL ** -0.5),
        "w_ff2": nrm(ks[28], (DEPTH, D_FF, D_MODEL), D_FF ** -0.5 * BETA),
        "ln3_g": 1.0 + nrm(ks[29], (DEPTH, D_MODEL), 0.02),
        "ln3_b": nrm(ks[30], (DEPTH, D_MODEL), 0.02),
    }


def reference(x_prompt, x_sample, mem_prompt, mem_sample, w_in, conv_w, conv_b, a_log_f, a_log_b,
              dt_bias_f, dt_bias_b, d_skip, ssd_norm_g, w_gk_f, b_gk_f, w_gk_b, b_gk_b, gla_norm_g, w_out,
              ln1_g, ln1_b, w_mq, w_mk, w_mv, w_mo, ln2_g, ln2_b, w_ff1, w_ff2, ln3_g, ln3_b):
    params = (w_in, conv_w, conv_b, a_log_f, a_log_b, dt_bias_f, dt_bias_b, d_skip, ssd_norm_g,
              w_gk_f, b_gk_f, w_gk_b, b_gk_b, gla_norm_g, w_out, ln1_g, ln1_b,
              w_mq, w_mk, w_mv, w_mo, ln2_g, ln2_b, w_ff1, w_ff2, ln3_g, ln3_b)
    y_prompt = run_trunk(x_prompt, mem_prompt, params)
    y_sample = run_trunk(x_sample, mem_sample, params)
    return (y_prompt, y_sample)
```

```python
import contextlib
import numpy as np
import concourse.bass as bass
import concourse.mybir as mybir
from concourse.bass_utils import run_bass_kernel_spmd

F32 = mybir.dt.float32
BF16 = mybir.dt.bfloat16
AF = mybir.ActivationFunctionType
ALU = mybir.AluOpType
AX = mybir.AxisListType

ENGS = ("pe", "act", "dve", "pool", "sp")
PIPE = {"A": True, "B": True}
SAME_ENGINE_SYNC = True

D = 1024
IN_PROJ = 5680
ALPHA = 2.0 ** 0.25
LN_EPS = 1e-5
RMS_EPS = 1e-5


class _Op:
    __slots__ = ("eng", "fn", "waits", "idx", "dma", "rank")


class Prog:
    def __init__(self, nc):
        self.nc = nc
        self.streams = {e: [] for e in ENGS}
        self.last_w = {}
        self.readers = {}
        self.seen = {e: {} for e in ENGS}
        self.pending = {e: [] for e in ENGS}
        self.dma_count = {}
        self.dma_last = {}
        self.sbuf_ptr = nc.sbuf_base
        self.sbuf_max = 0
        self.n_t = 0
        self.par = 0
        self.parity_keys = set()
        self.stream = None
        self.shared_keys = set()
        self.fset = 0
        self.fset_keys = set()

    def alloc(self, name, shape, dtype):
        nbytes = int(np.prod(shape[1:])) * mybir.dt.size(dtype)
        off = (self.sbuf_ptr + 31) // 32 * 32
        self.sbuf_ptr = off + nbytes
        self.sbuf_max = max(self.sbuf_max, self.sbuf_ptr)
        assert self.sbuf_ptr <= self.nc.sbuf_top, (name, self.sbuf_ptr)
        self.n_t += 1
        return self.nc.alloc_sbuf_tensor_at(f"{name}_{self.n_t}", list(shape), dtype, offset=off)

    def _dep(self, op, d):
        if d is None:
            return
        e = op.eng
        if d.dma is not None:
            key, val = d.dma
            if key.startswith("G:"):
                val = self.dma_count[key]
            if self.seen[e].get(("dma", key), 0) >= val:
                return
            op.waits.append((d, val))
            self.seen[e][("dma", key)] = val
        else:
            if d.eng == e:
                if e in ("pe", "sp") or not SAME_ENGINE_SYNC:
                    return
            if self.seen[e].get(d.eng, -1) >= d.idx:
                return
            op.waits.append((d, None))
            self.seen[e][d.eng] = d.idx

    def op(self, eng, fn, r=(), w=(), dma=None):
        if self.stream is not None:
            sfx = "#" + str(self.stream)
            sh = self.shared_keys
            r = [k if (k in sh or k.startswith("pb")) else k + sfx for k in r]
            w = [k if (k in sh or k.startswith("pb")) else k + sfx for k in w]
            if dma is not None:
                dma = dma + sfx
        if any(k.startswith("pb") for k in r):
            w = list(w) + [k for k in r if k.startswith("pb")]
            r = [k for k in r if not k.startswith("pb")]
        fk_ = self.fset_keys
        if fk_:
            r = [k + "$" + str(self.fset) if k in fk_ else k for k in r]
            w = [k + "$" + str(self.fset) if k in fk_ else k for k in w]
        pk_ = self.parity_keys
        if pk_:
            r = [k + "@" + str(self.par) if k in pk_ else k for k in r]
            w = [k + "@" + str(self.par) if k in pk_ else k for k in w]
        o = _Op()
        o.eng = eng
        o.fn = fn
        o.waits = []
        o.idx = len(self.streams[eng])
        o.rank = None
        if dma is not None:
            c = self.dma_count.get(dma, 0) + 16
            self.dma_count[dma] = c
            o.dma = (dma, c)
            self.dma_last[dma] = o
        else:
            o.dma = None
        for d in self.pending[eng]:
            self._dep(o, d)
        self.pending[eng] = []
        for k in r:
            self._dep(o, self.last_w.get(k))
        for k in w:
            self._dep(o, self.last_w.get(k))
            for rd in self.readers.get(k, ()):
                if rd is not o:
                    self._dep(o, rd)
        for k in r:
            self.readers.setdefault(k, []).append(o)
        for k in w:
            self.last_w[k] = o
            self.readers[k] = []
        self.streams[eng].append(o)
        return o

    def barrier(self):
        lasts = []
        for e in ENGS:
            if e != "sp" and self.streams[e]:
                lasts.append(self.streams[e][-1])
        lasts += list(self.dma_last.values())
        for e in ENGS:
            self.pending[e] = list(lasts)
        self.last_w = {}
        self.readers = {}

    def pe(self, fn, r=(), w=()):
        return self.op("pe", fn, r, w)

    def act(self, fn, r=(), w=()):
        return self.op("act", fn, r, w)

    def dve(self, fn, r=(), w=()):
        return self.op("dve", fn, r, w)

    def pool(self, fn, r=(), w=()):
        return self.op("pool", fn, r, w)

    def dma(self, out, in_, key, r=(), w=(), **kw):
        return self.op("sp", lambda e: e.dma_start(out=out, in_=in_, **kw), r, w, dma=key)

    def emit(self, final_wait_keys=()):
        nc = self.nc
        for e in ENGS:
            for o in self.streams[e]:
                for d, _ in o.waits:
                    if d.dma is None:
                        d.rank = 0
        finals = [self.dma_last[k] for k in final_wait_keys]
        for e in ENGS:
            if e == "sp":
                continue
            n = 0
            for o in self.streams[e]:
                if o.rank is not None:
                    n += 1
                    o.rank = n
        with contextlib.ExitStack() as st:
            esem = {e: st.enter_context(nc.semaphore(f"s_{e}")) for e in ENGS if e != "sp"}
            dsem = {k: st.enter_context(nc.semaphore(f"d_{i}")) for i, k in enumerate(self.dma_count)}
            block = st.enter_context(nc.Block())

            def run(eng_name):
                def body(eng):
                    for o in self.streams[eng_name]:
                        for d, v in o.waits:
                            if d.dma is not None:
                                eng.wait_ge(dsem[d.dma[0]], v)
                            else:
                                eng.wait_ge(esem[d.eng], d.rank)
                        ins = o.fn(eng)
                        if o.dma is not None:
                            ins.then_inc(dsem[o.dma[0]], 16)
                        elif o.rank is not None:
                            ins.then_inc(esem[o.eng], 1)
                    if eng_name == "sp":
                        for d in finals:
                            eng.wait_ge(dsem[d.dma[0]], d.dma[1])
                return body

            block.tensor(run("pe"))
            block.scalar(run("act"))
            block.vector(run("dve"))
            block.gpsimd(run("pool"))
            block.sync(run("sp"))


C_ID, C_MLF, C_MLB, C_SUF, C_SUB, C_ONE, C_TRF, C_TRB, C_S16F, C_S16B = [i * 128 for i in range(10)]
C_CW = 1280
C_CB = C_CW + 60
C_DTB = C_CB + 12
C_AL = C_DTB + 32
C_DS = C_AL + 32
NPA = C_DS + 16


def host_consts():
    k = np.arange(128)
    mlf = (k[:, None] <= k[None, :]).astype(np.float32)
    mlb = (k[:, None] >= k[None, :]).astype(np.float32)
    suf = (k[:, None] > k[None, :]).astype(np.float32)
    sub = (k[:, None] < k[None, :]).astype(np.float32)
    return np.concatenate([np.eye(128, dtype=np.float32), mlf, mlb, suf, sub, np.ones((128, 128), np.float32),
                           -mlf / 16.0, -mlb / 16.0, -suf / 16.0, -sub / 16.0], axis=1)


def build(seq_lens, mem_tokens=256):
    nc = bass.Bass("TRN2", target_bir_lowering=False)
    LT = sum(seq_lens)
    NS = len(seq_lens)
    MT = mem_tokens
    x_d = nc.dram_tensor("x", [LT, D], F32, kind="ExternalInput").ap()
    mem_d = nc.dram_tensor("mem", [NS * MT, D], F32, kind="ExternalInput").ap()
    w_in_d = nc.dram_tensor("w_in", [D, IN_PROJ], F32, kind="ExternalInput").ap()
    w_out_d = nc.dram_tensor("w_out", [2048, D], F32, kind="ExternalInput").ap()
    w_mq_d = nc.dram_tensor("w_mq", [D, D], F32, kind="ExternalInput").ap()
    w_mk_d = nc.dram_tensor("w_mk", [D, D], F32, kind="ExternalInput").ap()
    w_mv_d = nc.dram_tensor("w_mv", [D, D], F32, kind="ExternalInput").ap()
    w_mo_d = nc.dram_tensor("w_mo", [D, D], F32, kind="ExternalInput").ap()
    w_ff1_d = nc.dram_tensor("w_ff1", [D, 4096], F32, kind="ExternalInput").ap()
    w_ff2_d = nc.dram_tensor("w_ff2", [4096, D], F32, kind="ExternalInput").ap()
    prmA_d = nc.dram_tensor("prmA", [128, NPA], F32, kind="ExternalInput").ap()
    rowp_d = nc.dram_tensor("rowp", [17, 2560], F32, kind="ExternalInput").ap()
    prmB_d = nc.dram_tensor("prmB", [128, 8 * 1024], F32, kind="ExternalInput").ap()
    y_d = nc.dram_tensor("y", [LT, D], F32, kind="ExternalOutput").ap()
    ysc = nc.dram_tensor("ysc", [2, LT, D], F32).ap()
    osc = nc.dram_tensor("osc", [2, LT, D], F32).ap()
    x2sc = nc.dram_tensor("x2sc", [LT, D], F32).ap()
    packsc = nc.dram_tensor("packsc", [LT // 128, 128, 4352], BF16).ap()
    dtsc = nc.dram_tensor("dtsc", [LT // 128, 128, 16], F32).ap()
    lrsc = nc.dram_tensor("lrsc", [LT // 128, 16, 128], F32).ap()

    P = Prog(nc)
    st = contextlib.ExitStack()
    psum = st.enter_context(nc.psum_tensor("psum", [128, 4096], F32))
    pstate = {"i": 0, "g": None, "gi": [0, 0, 0, 0], "phase": "A", "groups": [(0, 4), (4, 4)]}

    def pb(n=1):
        g = pstate["g"]
        if g is None:
            lo, nbk, i = 0, 8, pstate["i"]
        else:
            lo, nbk = pstate["groups"][g]
            i = pstate["gi"][g]
        if n == 2 and i % 2:
            i += 1
        if i + n > nbk:
            i = 0
        if g is None:
            pstate["i"] = (i + n) % nbk
        else:
            pstate["gi"][g] = (i + n) % nbk
        i += lo
        return psum[:, i * 512:(i + n) * 512], [f"pb{j}" for j in range(i, i + n)]

    def run_streams(items, make_gen, K, stagger):
        pend = list(items)
        active = []
        free = list(range(K))
        since = 10 ** 9
        while pend or active:
            if pend and free and since >= stagger:
                slot = free.pop(0)
                active.append((make_gen(pend.pop(0), slot), slot))
                since = 0
            for it in list(active):
                g, slot = it
                pstate["g"] = slot
                P.stream = slot
                try:
                    next(g)
                except StopIteration:
                    active.remove(it)
                    free.append(slot)
            since += 1
        pstate["g"] = None
        P.stream = None

    def run_pipe(gens):
        live = list(gens)
        if not PIPE[pstate["phase"]]:
            for g, grp, par in live:
                pstate["g"] = grp
                P.par = par
                for _ in g:
                    pass
            pstate["g"] = None
            return
        while live:
            for it in list(live):
                g, grp, par = it
                pstate["g"] = grp
                P.par = par["par"] if isinstance(par, dict) else par
                P.fset = par.get("fset", 0) if isinstance(par, dict) else 0
                try:
                    next(g)
                except StopIteration:
                    while it in live:
                        live.remove(it)
        pstate["g"] = None

    cast_rr = {"i": 0}

    def cast(out, in_, r, w):
        i = cast_rr["i"]
        cast_rr["i"] = i + 1
        if i % 3 == 0:
            P.dve(lambda e: e.tensor_copy(out, in_), r=r, w=w)
        elif i % 3 == 1:
            P.act(lambda e: e.copy(out, in_), r=r, w=w)
        else:
            P.pool(lambda e: e.tensor_copy(out, in_), r=r, w=w)

    def load_weight(dst, src, kchunks, col0, ncols, dcol0, stg, tag):
        CH = 2048
        n = 0
        for kc in range(kchunks):
            for c in range(0, ncols, CH):
                cn = min(CH, ncols - c)
                s = stg["i"] % len(stg["t"])
                stg["i"] += 1
                sk = f"stg{s}"
                stt = stg["t"][s]
                P.dma(stt[:, 0:cn], src[kc * 128:(kc + 1) * 128, col0 + c:col0 + c + cn], sk, w=[sk])
                cast(dst[:, kc, dcol0 + c:dcol0 + c + cn], stt[:, 0:cn], r=[sk], w=[tag])
                n += 1

    base_ptr = P.sbuf_ptr

    P.sbuf_ptr = base_ptr
    wS = P.alloc("wS", [128, 8, 3632], BF16)
    diagw = P.alloc("diagw", [128, 12, 5, 128], BF16)
    DI = P.alloc("DI", [128, 16, 128], BF16)
    prmA = P.alloc("prmA", [128, NPA], F32)
    wgk = P.alloc("wgk", [17, 1024], F32)
    cbrow_f = P.alloc("cbrow_f", [1, 1536], F32)
    cbrow = P.alloc("cbrow", [1, 1536], BF16)
    identb = P.alloc("identb", [128, 128], BF16)
    onesb = P.alloc("onesb", [1, 128], BF16)
    aneg = P.alloc("aneg", [128, 32], F32)
    lrT = P.alloc("lrT", [17, 128], F32)
    hT = P.alloc("hT", [128, 1024], F32)
    hTb = P.alloc("hTb", [128, 1024], BF16)
    Hs = P.alloc("Hs", [128, 4, 256], F32)
    Hsb = P.alloc("Hsb", [128, 4, 256], BF16)
    markA = P.sbuf_ptr
    stg = {"i": 0, "t": [P.alloc(f"stg{i}", [128, 2048], F32) for i in range(6)]}

    P.dma(prmA[:], prmA_d[:, :], "G:A", w=["prmA"])
    P.dma(wgk[:], rowp_d[:, 0:1024], "G:A", w=["wgk"])
    P.dma(cbrow_f[:], rowp_d[0:1, 1024:2560], "G:A", w=["cbrow_f"])
    load_weight(wS, w_in_d, 8, 1024, 3616, 0, stg, "wS")
    load_weight(wS, w_in_d, 8, 5664, 16, 3616, stg, "wS")
    ident = prmA[:, C_ID:C_ID + 128]
    ones = prmA[:, C_ONE:C_ONE + 128]
    P.dve(lambda e: e.tensor_copy(identb[:], ident), r=["prmA"], w=["identb"])
    P.dve(lambda e: e.tensor_copy(onesb[:], prmA[0:1, C_ONE:C_ONE + 128]), r=["prmA"], w=["onesb"])
    P.dve(lambda e: e.tensor_copy(cbrow[:], cbrow_f[:]), r=["cbrow_f"], w=["cbrow"])
    for c in range(12):
        for k in range(5):
            col = C_CW + c * 5 + k
            eng = P.dve if (c * 5 + k) % 2 == 0 else P.pool
            eng(lambda e, c=c, k=k, col=col: e.tensor_scalar(out=diagw[:, c, k, :], in0=ident, scalar1=prmA[:, col:col + 1],
                                                             scalar2=None, op0=ALU.mult), r=["prmA"], w=["diagw"])
    for h in range(16):
        P.dve(lambda e, h=h: e.tensor_scalar(out=DI[:, h, :], in0=ident, scalar1=prmA[:, C_DS + h:C_DS + h + 1],
                                             scalar2=None, op0=ALU.mult), r=["prmA"], w=["DI"])
    P.act(lambda e: e.activation(out=aneg[:], in_=prmA[:, C_AL:C_AL + 32], func=AF.Exp), r=["prmA"], w=["aneg"])
    P.dve(lambda e: e.tensor_scalar(out=aneg[:], in0=aneg[:], scalar1=-1.0, scalar2=None, op0=ALU.mult), r=["aneg"], w=["aneg"])
    P.pool(lambda e: e.memset(lrT[:], 1.0), w=["lrT"])
    P.barrier()
    P.sbuf_ptr = markA

    xt = [P.alloc(f"xt{i}", [128, 1024], F32) for i in range(2)]
    xh = [P.alloc(f"xh{i}", [4, 1024], F32) for i in range(2)]
    xT = P.alloc("xT", [128, 8, 132], BF16)
    uT = P.alloc("uT", [128, 12, 132], BF16)
    pack2 = [P.alloc("pack", [128, 4352], BF16) for _ in range(2)]
    xs_tok2 = [pk_[:, 0:1024] for pk_ in pack2]
    B_tok2 = [pk_[:, 1024:1280] for pk_ in pack2]
    BCT2 = [pk_[:, 1280:1792].rearrange("p (c l) -> p c l", c=4) for pk_ in pack2]
    v_bf2 = [pk_[:, 1792:2816] for pk_ in pack2]
    qraw2 = [pk_[:, 2816:3328] for pk_ in pack2]
    kTraw2 = [pk_[:, 3328:3840] for pk_ in pack2]
    ktok2 = [pk_[:, 3840:4352] for pk_ in pack2]
    dtraw2 = [P.alloc("dtraw", [128, 16], F32) for _ in range(2)]
    sm2 = [P.alloc("sm", [128, 128], F32) for _ in range(2)]
    Rt = P.alloc("Rt", [128, 16, 128], F32)
    Et = P.alloc("Et", [128, 16, 128], BF16)
    cbm = P.alloc("cbm", [128, 2, 128], BF16)
    Mt = P.alloc("Mt", [128, 16, 128], BF16)
    xdt = P.alloc("xdt", [128, 1024], BF16)
    xdtd = P.alloc("xdtd", [128, 1024], BF16)
    ytmp = P.alloc("ytmp", [128, 1024], F32)
    yout = [P.alloc(f"yout{i}", [128, 1024], F32) for i in range(2)]
    g_e = P.alloc("g_e", [128, 512], F32)
    g_sp = P.alloc("g_sp", [128, 512], F32)
    g_eb2 = [P.alloc("g_eb", [128, 4, 128], F32) for _ in range(2)]
    g_enb = P.alloc("g_enb", [128, 4, 128], F32)
    g_ed = P.alloc("g_ed", [128, 512], F32)
    qtT2 = [P.alloc("qtT", [128, 4, 128], BF16) for _ in range(2)]
    ktT2 = [P.alloc("ktT", [128, 4, 128], BF16) for _ in range(2)]
    kend2 = [P.alloc("kend", [128, 512], BF16) for _ in range(2)]
    attm = P.alloc("attm", [128, 4, 128], BF16)
    oout = [P.alloc(f"oout{i}", [128, 1024], F32) for i in range(2)]

    def issue_x_load(tok0, seq_lo, seq_hi, slot, halo=True):
        P.dma(xt[slot][:], x_d[tok0:tok0 + 128, :], f"xt{slot}", w=[f"xt{slot}"])
        if not halo:
            return
        hk = f"xh{slot}"
        lo_ok = tok0 - 2 >= seq_lo
        hi_ok = tok0 + 130 <= seq_hi
        if not (lo_ok and hi_ok):
            P.pool(lambda e: e.memset(xh[slot][:], 0.0), w=[hk])
        if lo_ok:
            P.dma(xh[slot][0:2, :], x_d[tok0 - 2:tok0, :], hk, w=[hk])
        if hi_ok:
            P.dma(xh[slot][2:4, :], x_d[tok0 + 128:tok0 + 130, :], hk, w=[hk])

    def transpose_x(slot, dst, c0, identity=ident, idk="prmA"):
        for half in range(2):
            pt, pk = pb()
            for j in range(4):
                kc = half * 4 + j
                P.pe(lambda e, kc=kc, j=j, pt=pt: e.transpose(pt[:, j * 128:(j + 1) * 128], xt[slot][:, kc * 128:(kc + 1) * 128], identity),
                     r=[f"xt{slot}", idk], w=pk)
            src = pt.rearrange("p (c n) -> p c n", n=128)
            if half == 0:
                P.dve(lambda e, src=src: e.tensor_copy(dst[:, 0:4, c0:c0 + 128], src), r=pk, w=["xT"])
            else:
                P.act(lambda e, src=src: e.copy(dst[:, 4:8, c0:c0 + 128], src), r=pk, w=["xT"])

    tile_ctr = {"i": 0}
    Mt2 = [Mt, xt[0][:].bitcast(BF16).rearrange("p (h l) -> p h l", h=16)]
    xdt2 = [xdt, xt[1][:].bitcast(BF16)[:, 0:1024]]
    xdtd2 = [xdtd, xt[1][:].bitcast(BF16)[:, 1024:2048]]
    attm2 = [attm, xT[:].rearrange("p c n -> p (c n)")[:, 0:512].rearrange("p (h l) -> p h l", h=4)]
    wSf = wS[:].rearrange("p a b -> p (a b)")
    def carve(off, n):
        return wSf[:, off:off + n]
    pack3 = carve(0, 4352)
    third = dict(
        sm=carve(4352, 256).bitcast(F32),
        g_eb=carve(4608, 1024).bitcast(F32).rearrange("p (h l) -> p h l", h=4),
        qtT=carve(5632, 512).rearrange("p (h l) -> p h l", h=4),
        ktT=carve(6144, 512).rearrange("p (h l) -> p h l", h=4),
        kend=carve(6656, 512),
        dtraw=carve(7168, 32).bitcast(F32),
        Mt=carve(7232, 2048).rearrange("p (h l) -> p h l", h=16),
        xdt=carve(9280, 1024), xdtd=carve(10304, 1024),
        attm=carve(11328, 512).rearrange("p (h l) -> p h l", h=4))
    g_eF = [g_e, carve(11840, 1024).bitcast(F32)]
    g_spF = [g_sp, carve(12864, 1024).bitcast(F32)]
    g_enbF = [g_enb, carve(13888, 1024).bitcast(F32).rearrange("p (h l) -> p h l", h=4)]
    g_edF = [g_ed, carve(14912, 1024).bitcast(F32)]
    RtF = [Rt, carve(15936, 4096).bitcast(F32).rearrange("p (h l) -> p h l", h=16)]
    EtF = [Et, carve(20032, 2048).rearrange("p (h l) -> p h l", h=16)]
    cbmF = [cbm, carve(22080, 256).rearrange("p (g l) -> p g l", g=2)]
    lrT_b = carve(22336, 256).bitcast(F32)
    lrTF = [lrT, lrT_b]
    pack2.append(pack3)
    xs_tok2.append(pack3[:, 0:1024])
    B_tok2.append(pack3[:, 1024:1280])
    BCT2.append(pack3[:, 1280:1792].rearrange("p (c l) -> p c l", c=4))
    v_bf2.append(pack3[:, 1792:2816])
    qraw2.append(pack3[:, 2816:3328])
    kTraw2.append(pack3[:, 3328:3840])
    ktok2.append(pack3[:, 3840:4352])
    sm2.append(third["sm"]); g_eb2.append(third["g_eb"]); qtT2.append(third["qtT"]); ktT2.append(third["ktT"])
    kend2.append(third["kend"]); dtraw2.append(third["dtraw"])
    Mt2.append(third["Mt"]); xdt2.append(third["xdt"]); xdtd2.append(third["xdtd"]); attm2.append(third["attm"])
    P.parity_keys = {"qraw", "kTraw", "ktokraw", "dtraw", "xs_tok", "B_tok", "BCT", "v_bf", "qtT", "ktT", "kend", "g_eb", "sm_dtv", "sm_dte", "sm_dt", "sm_dta"}

    def scan_front(tok0, slot, d, par, load, fs=0):
        MLE = prmA[:, (C_MLF if d == 0 else C_MLB):(C_MLF if d == 0 else C_MLB) + 128]
        SU = prmA[:, (C_SUF if d == 0 else C_SUB):(C_SUF if d == 0 else C_SUB) + 128]
        TR16 = prmA[:, (C_TRF if d == 0 else C_TRB):(C_TRF if d == 0 else C_TRB) + 128]
        SU16 = prmA[:, (C_S16F if d == 0 else C_S16B):(C_S16F if d == 0 else C_S16B) + 128]
        xs_tok = xs_tok2[par]
        B_tok = B_tok2[par]
        BCT = BCT2[par]
        sm = sm2[par]
        g_eb = g_eb2[par]
        qtT = qtT2[par]
        ktT = ktT2[par]
        kend = kend2[par]
        v_bf = v_bf2[par]
        qraw, kTraw, ktokraw, dtraw = qraw2[par], kTraw2[par], ktok2[par], dtraw2[par]
        g_e, g_sp, g_enb, g_ed, lrT = g_eF[fs], g_spF[fs], g_enbF[fs], g_edF[fs], lrTF[fs]
        tix = tok0 // 128
        PK = ["xs_tok", "B_tok", "BCT", "v_bf", "qraw", "kTraw", "ktokraw"]
        xk, hk = f"xt{slot}", f"xh{slot}"
        if load:
            P.dma(pack2[par][:], packsc[tix], f"pack{par}", w=PK)
            P.dma(lrT[0:16, :], lrsc[tix], f"lrTl{fs}", w=["lrT"])
            P.dma(dtraw[:], dtsc[tix], f"dtraw{par}", w=["dtraw"])
            yield
        else:
            transpose_x(slot, xT, 2)
            pt, pk = pb()
            for kc in range(8):
                P.pe(lambda e, kc=kc, pt=pt: e.transpose(pt[:, kc * 4:(kc + 1) * 4], xh[slot][0:4, kc * 128:(kc + 1) * 128], ident[0:4, 0:4]),
                     r=[hk, "prmA"], w=pk)
            hv = pt[:, 0:32].rearrange("p (c n) -> p c n", n=4)
            P.dve(lambda e, hv=hv: e.tensor_copy(xT[:, :, 0:2], hv[:, :, 0:2]), r=pk, w=["xT"])
            P.dve(lambda e, hv=hv: e.tensor_copy(xT[:, :, 130:132], hv[:, :, 2:4]), r=pk, w=["xT"])
            yield
            for grp in range(4):
                pt, pk = pb()
                for j in range(3):
                    c = grp * 3 + j
                    for kc in range(8):
                        P.pe(lambda e, c=c, kc=kc, j=j, pt=pt: e.matmul(pt[:, j * 132:(j + 1) * 132], wS[:, kc, c * 128:(c + 1) * 128],
                                                                       xT[:, kc, :], start=(kc == 0), stop=(kc == 7)),
                             r=["wS", "xT"], w=pk)
                src = pt[:, 0:396].rearrange("p (c n) -> p c n", n=132)
                if grp % 2 == 0:
                    P.act(lambda e, src=src, grp=grp: e.copy(uT[:, grp * 3:grp * 3 + 3, :], src), r=pk, w=["uT"])
                else:
                    P.dve(lambda e, src=src, grp=grp: e.tensor_copy(uT[:, grp * 3:grp * 3 + 3, :], src), r=pk, w=["uT"])
                yield
            psml, psmlk = pb()
            for kc in range(8):
                P.pe(lambda e, kc=kc: e.matmul(psml[0:16, 0:128], wS[:, kc, 3616:3632], xT[:, kc, 2:130], start=(kc == 0), stop=(kc == 7)),
                     r=["wS", "xT"], w=psmlk)
            P.act(lambda e: e.copy(lrT[0:16, :], psml[0:16, 0:128]), r=psmlk, w=["lrT"])
            yield
            P.dma(lrsc[tix], lrT[0:16, :], "lrTs", r=["lrT"])
        pg, pgk = pb()
        P.pe(lambda e: e.matmul(pg[:, 0:512], lrT[0:17, :], wgk[0:17, 512 * d:512 * d + 512], start=True, stop=True),
             r=["lrT", "wgk"], w=pgk)
        P.act(lambda e: e.activation(out=g_e[:], in_=pg[:, 0:512], func=AF.Exp, scale=-1.0), r=pgk, w=["g_e"])
        P.act(lambda e: e.activation(out=g_sp[:], in_=g_e[:], func=AF.Ln, bias=1.0), r=["g_e"], w=["g_sp"])
        yield
        pbT, pbTk = pb()
        for h in range(4):
            P.pe(lambda e, h=h: e.matmul(pbT[:, h * 128:(h + 1) * 128], g_sp[:, h * 128:(h + 1) * 128], TR16, start=True, stop=True),
                 r=["g_sp", "prmA"], w=pbTk)
        pde, pdek = pb()
        P.pe(lambda e: e.matmul(pde[:, 0:512], SU16, g_sp[:], start=True, stop=True), r=["g_sp", "prmA"], w=pdek)
        P.act(lambda e: e.activation(out=g_eb[:].rearrange("p h l -> p (h l)"), in_=pbT[:, 0:512], func=AF.Exp), r=pbTk, w=["g_eb"])
        P.act(lambda e: e.activation(out=g_enb[:].rearrange("p h l -> p (h l)"), in_=pbT[:, 0:512], func=AF.Exp, scale=-1.0), r=pbTk, w=["g_enb"])
        P.act(lambda e: e.activation(out=g_ed[:], in_=pde[:, 0:512], func=AF.Exp), r=pdek, w=["g_ed"])
        yield
        dtv, dte, dtt, dta = sm[:, 0:16], sm[:, 16:32], sm[:, 32:48], sm[:, 48:64]
        if load:
            P.dve(lambda e: e.scalar_tensor_tensor(out=qtT[:].rearrange("p h l -> p (h l)"), in0=qraw, scalar=128.0 ** -0.5,
                                                   in1=g_eb[:].rearrange("p h l -> p (h l)"), op0=ALU.mult, op1=ALU.mult),
                  r=["qraw", "g_eb"], w=["qtT"])
            P.pool(lambda e: e.tensor_tensor(out=ktT[:].rearrange("p h l -> p (h l)"), in0=kTraw,
                                             in1=g_enb[:].rearrange("p h l -> p (h l)"), op=ALU.mult), r=["kTraw", "g_enb"], w=["ktT"])
            P.dve(lambda e: e.tensor_tensor(out=kend[:], in0=ktokraw, in1=g_ed[:], op=ALU.mult), r=["ktokraw", "g_ed"], w=["kend"])
            yield
            P.dve(lambda e: e.tensor_tensor(out=dtv, in0=dtraw[:], in1=prmA[:, C_DTB + 16 * d:C_DTB + 16 * d + 16], op=ALU.add),
                  r=["dtraw", "prmA"], w=["sm_dtv"])
        else:
            pq, pqk = pb()
            for h in range(4):
                for kc in range(8):
                    P.pe(lambda e, h=h, kc=kc: e.matmul(pq[:, h * 128:(h + 1) * 128], wS[:, kc, 1568 + h * 128:1568 + (h + 1) * 128],
                                                       xT[:, kc, 2:130], start=(kc == 0), stop=(kc == 7)), r=["wS", "xT"], w=pqk)
            P.dve(lambda e: e.scalar_tensor_tensor(out=qtT[:].rearrange("p h l -> p (h l)"), in0=pq[:, 0:512], scalar=128.0 ** -0.5,
                                                   in1=g_eb[:].rearrange("p h l -> p (h l)"), op0=ALU.mult, op1=ALU.mult),
                  r=pqk + ["g_eb"], w=["qtT"])
            P.act(lambda e: e.copy(qraw, pq[:, 0:512]), r=pqk, w=["qraw"])
            yield
            pkT, pkTk = pb()
            for h in range(4):
                for kc in range(8):
                    P.pe(lambda e, h=h, kc=kc: e.matmul(pkT[:, h * 128:(h + 1) * 128], wS[:, kc, 2080 + h * 128:2080 + (h + 1) * 128],
                                                       xT[:, kc, 2:130], start=(kc == 0), stop=(kc == 7)), r=["wS", "xT"], w=pkTk)
            P.dve(lambda e: e.tensor_tensor(out=ktT[:].rearrange("p h l -> p (h l)"), in0=pkT[:, 0:512],
                                            in1=g_enb[:].rearrange("p h l -> p (h l)"), op=ALU.mult), r=pkTk + ["g_enb"], w=["ktT"])
            P.act(lambda e: e.copy(kTraw, pkT[:, 0:512]), r=pkTk, w=["kTraw"])
            yield
            pkt, pktk = pb()
            for kc in range(8):
                P.pe(lambda e, kc=kc: e.matmul(pkt[:, 0:512], xT[:, kc, 2:130], wS[:, kc, 2080:2592], start=(kc == 0), stop=(kc == 7)),
                     r=["wS", "xT"], w=pktk)
            P.dve(lambda e: e.tensor_tensor(out=kend[:], in0=pkt[:, 0:512], in1=g_ed[:], op=ALU.mult), r=pktk + ["g_ed"], w=["kend"])
            P.act(lambda e: e.copy(ktokraw, pkt[:, 0:512]), r=pktk, w=["ktokraw"])
            yield
            pdt, pdtk = pb()
            for kc in range(8):
                P.pe(lambda e, kc=kc: e.matmul(pdt[:, 0:32], xT[:, kc, 2:130], wS[:, kc, 1536:1568], start=(kc == 0), stop=(kc == 7)),
                     r=["wS", "xT"], w=pdtk)
            P.dve(lambda e: e.tensor_tensor(out=dtv, in0=pdt[:, 16 * d:16 * d + 16], in1=prmA[:, C_DTB + 16 * d:C_DTB + 16 * d + 16], op=ALU.add),
                  r=pdtk + ["prmA"], w=["sm_dtv"])
            P.act(lambda e: e.copy(dtraw[:], pdt[:, 16 * (1 - d):16 * (1 - d) + 16]), r=pdtk, w=["dtraw"])
            P.dma(dtsc[tix], dtraw[:], f"dtraws{par}", r=["dtraw"])
        P.act(lambda e: e.activation(out=dte, in_=dtv, func=AF.Exp), r=["sm_dtv"], w=["sm_dte"])
        P.act(lambda e: e.activation(out=dtt, in_=dte, func=AF.Ln, bias=1.0), r=["sm_dte"], w=["sm_dt"])
        P.dve(lambda e: e.tensor_tensor(out=dta, in0=dtt, in1=aneg[:, 16 * d:16 * d + 16], op=ALU.mult), r=["sm_dt", "aneg"], w=["sm_dta"])
        if not load:
            yield
            pv, pvk = pb(2)
            for half in range(2):
                for kc in range(8):
                    P.pe(lambda e, kc=kc, half=half: e.matmul(pv[:, half * 512:(half + 1) * 512], xT[:, kc, 2:130],
                                                             wS[:, kc, 2592 + half * 512:2592 + (half + 1) * 512], start=(kc == 0), stop=(kc == 7)),
                         r=["wS", "xT"], w=[pvk[half]])
            P.act(lambda e: e.copy(v_bf[:], pv), r=pvk, w=["v_bf"])
            yield
            pcx, pcxk = pb(2)
            for c in range(8):
                o = pcx[:, c * 128:(c + 1) * 128]
                for k in range(5):
                    P.pe(lambda e, o=o, c=c, k=k: e.matmul(o, uT[:, c, k:k + 128], diagw[:, c, k, :], start=(k == 0), stop=False),
                         r=["uT", "diagw"], w=[pcxk[c // 4]])
                P.pe(lambda e, o=o, c=c: e.matmul(o, onesb[0:1, :], cbrow[0:1, c * 128:(c + 1) * 128], start=False, stop=True),
                     r=["onesb", "cbrow"], w=[pcxk[c // 4]])
            yield
            pcb, pcbk = pb()
            for c in range(8, 10):
                o = pcb[:, (c - 8) * 128:(c - 7) * 128]
                for k in range(5):
                    P.pe(lambda e, o=o, c=c, k=k: e.matmul(o, uT[:, c, k:k + 128], diagw[:, c, k, :], start=(k == 0), stop=False),
                         r=["uT", "diagw"], w=pcbk)
                P.pe(lambda e, o=o, c=c: e.matmul(o, onesb[0:1, :], cbrow[0:1, c * 128:(c + 1) * 128], start=False, stop=True),
                     r=["onesb", "cbrow"], w=pcbk)
            pct, pctk = pb()
            for c in range(8, 12):
                o = pct[:, (c - 8) * 128:(c - 7) * 128]
                for k in range(5):
                    P.pe(lambda e, o=o, c=c, k=k: e.matmul(o, diagw[:, c, k, :], uT[:, c, k:k + 128], start=(k == 0), stop=(k == 4)),
                         r=["uT", "diagw"], w=pctk)
            yield
            P.act(lambda e: e.activation(out=xs_tok[:], in_=pcx, func=AF.Silu), r=pcxk, w=["xs_tok"])
            P.act(lambda e: e.activation(out=B_tok[:], in_=pcb[:, 0:256], func=AF.Silu), r=pcbk, w=["B_tok"])
            for c in range(8, 12):
                P.act(lambda e, c=c: e.activation(out=BCT[:, c - 8, :], in_=pct[:, (c - 8) * 128:(c - 7) * 128], func=AF.Silu,
                                                  bias=prmA[:, C_CB + c:C_CB + c + 1]), r=pctk + ["prmA"], w=["BCT"])
            P.dma(packsc[tix], pack2[par][:], f"packs{par}", r=PK)

    def scan_back(tok0, slot, d, par, add_skip, part=None, bpar=0, fs=0):
        MLE = prmA[:, (C_MLF if d == 0 else C_MLB):(C_MLF if d == 0 else C_MLB) + 128]
        SU = prmA[:, (C_SUF if d == 0 else C_SUB):(C_SUF if d == 0 else C_SUB) + 128]
        TR16 = prmA[:, (C_TRF if d == 0 else C_TRB):(C_TRF if d == 0 else C_TRB) + 128]
        SU16 = prmA[:, (C_S16F if d == 0 else C_S16B):(C_S16F if d == 0 else C_S16B) + 128]
        xs_tok = xs_tok2[par]
        B_tok = B_tok2[par]
        BCT = BCT2[par]
        sm = sm2[par]
        g_eb = g_eb2[par]
        qtT = qtT2[par]
        ktT = ktT2[par]
        kend = kend2[par]
        v_bf = v_bf2[par]
        dtv, dte, dtt, dta = sm[:, 0:16], sm[:, 16:32], sm[:, 32:48], sm[:, 48:64]
        Mt, xdt, xdtd, attm = Mt2[bpar], xdt2[bpar], xdtd2[bpar], attm2[bpar]
        Rt, Et, cbm = RtF[fs], EtF[fs], cbmF[fs]

        def part_a():
            pss, pssk = pb()
            P.pe(lambda e: e.matmul(pss[:, 0:16], SU, dta, start=True, stop=True), r=["prmA", "sm_dta"], w=pssk)
            P.pe(lambda e: e.matmul(pss[:, 16:32], MLE, dta, start=True, stop=True), r=["prmA", "sm_dta"], w=pssk)
            P.pe(lambda e: e.matmul(pss[:, 32:48], ones, dta, start=True, stop=True), r=["prmA", "sm_dta"], w=pssk)
            ex = sm[:, 64:112]
            P.act(lambda e: e.activation(out=ex, in_=pss[:, 0:48], func=AF.Exp), r=pssk, w=["sm_ex"])
            wdec = sm[:, 112:128]
            P.dve(lambda e: e.tensor_tensor(out=wdec, in0=dtt, in1=sm[:, 64:80], op=ALU.mult), r=["sm_dt", "sm_ex"], w=["sm_w"])
            yield
            P.dve(lambda e: e.tensor_tensor(out=Rt[:], in0=dta.unsqueeze(2).to_broadcast([128, 16, 128]),
                                            in1=MLE.unsqueeze(1).to_broadcast([128, 16, 128]), op=ALU.mult),
                  r=["sm_dta", "prmA"], w=["Rt"])
            yield
            pcbT, pcbTk = pb()
            for g in range(2):
                P.pe(lambda e, g=g: e.matmul(pcbT[:, g * 128:(g + 1) * 128], BCT[:, g, :], BCT[:, 2 + g, :], start=True, stop=True),
                     r=["BCT"], w=pcbTk)
            P.dve(lambda e: e.tensor_tensor(out=cbm[:], in0=pcbT[:, 0:256].rearrange("p (g l) -> p g l", g=2),
                                            in1=MLE.unsqueeze(1).to_broadcast([128, 2, 128]), op=ALU.mult),
                  r=pcbTk + ["prmA"], w=["cbm"])
            yield
            P.dve(lambda e: e.tensor_tensor(out=xdt[:].rearrange("p (h c) -> p h c", c=64), in0=xs_tok[:].rearrange("p (h c) -> p h c", c=64),
                                            in1=dtt.unsqueeze(2).to_broadcast([128, 16, 64]), op=ALU.mult),
                  r=["xs_tok", "sm_dt"], w=["xdt"])
            P.pool(lambda e: e.tensor_tensor(out=xdtd[:].rearrange("p (h c) -> p h c", c=64), in0=xs_tok[:].rearrange("p (h c) -> p h c", c=64),
                                             in1=wdec.unsqueeze(2).to_broadcast([128, 16, 64]), op=ALU.mult),
                   r=["xs_tok", "sm_w"], w=["xdtd"])
            yield
            for half in range(2):
                psg, psgk = pb(2)
                for q in range(2):
                    hh = half * 8 + q * 4
                    P.pe(lambda e, q=q, hh=hh, psg=psg: e.matmul(psg[:, q * 512:(q + 1) * 512], SU,
                                                                Rt[:, hh:hh + 4, :].rearrange("p h l -> p (h l)"), start=True, stop=True),
                         r=["prmA", "Rt"], w=[psgk[q]])
                P.act(lambda e, half=half, psg=psg: e.activation(out=Et[:, half * 8:(half + 1) * 8, :].rearrange("p h l -> p (h l)"), in_=psg,
                                                                func=AF.Exp), r=psgk, w=[f"Et{half}"])
                (P.pool if (d == 0 and half == 1) else P.dve)(lambda e, half=half: e.tensor_tensor(out=Mt[:, half * 8:(half + 1) * 8, :], in0=Et[:, half * 8:(half + 1) * 8, :],
                                                           in1=cbm[:, half, :].unsqueeze(1).to_broadcast([128, 8, 128]), op=ALU.mult),
                      r=[f"Et{half}", "cbm"], w=[f"Mt{half}"])
                yield
            yield
            pat, patk = pb()
            for h in range(4):
                P.pe(lambda e, h=h: e.matmul(pat[:, h * 128:(h + 1) * 128], ktT[:, h, :], qtT[:, h, :], start=True, stop=True),
                     r=["ktT", "qtT"], w=patk)
            P.dve(lambda e: e.tensor_tensor(out=attm[:], in0=pat[:, 0:512].rearrange("p (h l) -> p h l", h=4),
                                            in1=MLE.unsqueeze(1).to_broadcast([128, 4, 128]), op=ALU.mult), r=patk + ["prmA"], w=["attm"])

        def part_b1():
            yield
            pyo, pyok = pb(2)
            for g in range(2):
                P.pe(lambda e, g=g: e.matmul(pyo[:, g * 512:(g + 1) * 512], BCT[:, 2 + g, :], hTb[:, g * 512:(g + 1) * 512], start=True, stop=True),
                     r=["BCT", "hTb"], w=[pyok[g]])
            eacs = sm[:, 80:96]
            P.dve(lambda e: e.tensor_tensor(out=ytmp[:].rearrange("p (h c) -> p h c", c=64), in0=pyo.rearrange("p (h c) -> p h c", c=64),
                                            in1=eacs.unsqueeze(2).to_broadcast([128, 16, 64]), op=ALU.mult),
                  r=pyok + ["sm_ex"], w=["ytmp"])
            yield
            pyd, pydk = pb(2)
            for h in range(16):
                o = pyd[:, h * 64:(h + 1) * 64]
                P.pe(lambda e, o=o, h=h: e.matmul(o, Mt[:, h, :], xdt[:, h * 64:(h + 1) * 64], start=True, stop=not add_skip),
                     r=[f"Mt{h // 8}", "xdt"], w=[pydk[h // 8]])
                if add_skip:
                    P.pe(lambda e, o=o, h=h: e.matmul(o, DI[:, h, :], xs_tok[:, h * 64:(h + 1) * 64], start=False, stop=True),
                         r=["DI", "xs_tok"], w=[pydk[h // 8]])
            yield
            yo = yout[slot]
            yk = f"yout{slot}"
            P.dve(lambda e: e.tensor_tensor(out=yo[:], in0=ytmp[:], in1=pyd, op=ALU.add), r=["ytmp"] + pydk, w=[yk])
            P.dma(ysc[d, tok0:tok0 + 128, :], yo[:], yk + "s", r=[yk])
            yield
            pcs, pcsk = pb(2)
            for g in range(2):
                P.pe(lambda e, g=g: e.matmul(pcs[:, g * 512:(g + 1) * 512], B_tok[:, g * 128:(g + 1) * 128], xdtd[:, g * 512:(g + 1) * 512],
                                             start=True, stop=True), r=["B_tok", "xdtd"], w=[pcsk[g]])
            dec = sm[:, 96:112]
            (P.pool if d == 0 else P.dve)(lambda e: e.tensor_tensor(out=hT[:].rearrange("p (h c) -> p h c", c=64), in0=hT[:].rearrange("p (h c) -> p h c", c=64),
                                            in1=dec.unsqueeze(2).to_broadcast([128, 16, 64]), op=ALU.mult),
                  r=["sm_ex"], w=["hT"])
            P.dve(lambda e: e.tensor_tensor(out=hT[:], in0=hT[:], in1=pcs, op=ALU.add), r=pcsk, w=["hT"])
            P.act(lambda e: e.copy(hTb[:], hT[:]), r=["hT"], w=["hTb"])

        def part_b2():
            yield
            po, pok = pb(2)
            for h in range(4):
                o = po[:, h * 256:(h + 1) * 256]
                P.pe(lambda e, o=o, h=h: e.matmul(o, attm[:, h, :], v_bf[:, h * 256:(h + 1) * 256], start=True, stop=False),
                     r=["attm", "v_bf"], w=[pok[h // 2]])
                P.pe(lambda e, o=o, h=h: e.matmul(o, qtT[:, h, :], Hsb[:, h, :], start=False, stop=True),
                     r=["qtT", "Hsb"], w=[pok[h // 2]])
            oo = oout[slot]
            ok = f"oout{slot}"
            P.act(lambda e: e.copy(oo[:], po), r=pok, w=[ok])
            P.dma(osc[d, tok0:tok0 + 128, :], oo[:], ok + "s", r=[ok])
            yield
            pgs, pgsk = pb(2)
            for h in range(4):
                P.pe(lambda e, h=h: e.matmul(pgs[:, h * 256:(h + 1) * 256], kend[:, h * 128:(h + 1) * 128], v_bf[:, h * 256:(h + 1) * 256],
                                             start=True, stop=True), r=["kend", "v_bf"], w=[pgsk[h // 2]])
            last = 127 if d == 0 else 0
            for h in range(4):
                P.dve(lambda e, h=h: e.scalar_tensor_tensor(out=Hs[:, h, :], in0=Hs[:, h, :], scalar=g_eb[:, h, last:last + 1],
                                                            in1=pgs[:, h * 256:(h + 1) * 256], op0=ALU.mult, op1=ALU.add),
                      r=["g_eb", pgsk[h // 2]], w=["Hs"])
            P.act(lambda e: e.copy(Hsb[:].rearrange("p h v -> p (h v)"), Hs[:].rearrange("p h v -> p (h v)")), r=["Hs"], w=["Hsb"])


        if part in (None, "a"):
            yield from part_a()
        if part in (None, "b", "b1"):
            yield from part_b1()
        if part in (None, "b", "b2"):
            yield from part_b2()

    seq_offs = [sum(seq_lens[:i]) for i in range(NS)]
    def chain(*gs):
        for g_ in gs:
            yield from g_

    for d in (1, 0):
        if d == 0:
            P.barrier()
            pstate["groups"] = [(0, 2), (2, 2), (4, 2), (6, 2)]
            pstate["gi"] = [0, 0, 0, 0]
            P.fset_keys = {"g_e", "g_sp", "g_enb", "g_ed", "lrT", "Rt", "Et0", "Et1", "cbm"}
            P.pool(lambda e: e.memset(lrT_b[0:32, :], 1.0), w=["lrT$1"])
            P.parity_keys = P.parity_keys | {"Mt0", "Mt1", "xdt", "xdtd", "attm", "sm_ex", "sm_w"}
        for si in range(NS):
            L = seq_lens[si]
            so = seq_offs[si]
            nt = L // 128
            order = list(range(nt)) if d == 0 else list(range(nt - 1, -1, -1))
            P.pool(lambda e: e.memset(hT[:], 0.0), w=["hT"])
            P.pool(lambda e: e.memset(hTb[:], 0.0), w=["hTb"])
            P.pool(lambda e: e.memset(Hs[:].rearrange("p h v -> p (h v)"), 0.0), w=["Hs"])
            P.pool(lambda e: e.memset(Hsb[:].rearrange("p h v -> p (h v)"), 0.0), w=["Hsb"])
            base = tile_ctr["i"]
            if d == 1:
                issue_x_load(so + order[0] * 128, so, so + L, base % 2)
                if nt > 1:
                    issue_x_load(so + order[1] * 128, so, so + L, (base + 1) % 2)
                run_pipe([(scan_front(so + order[0] * 128, base % 2, d, base % 2, False), 0, base % 2)])
                for j, ti in enumerate(order):
                    slot = (base + j) % 2
                    gens = [(scan_back(so + ti * 128, slot, d, slot, False), 1, slot)]
                    if j + 1 < nt:
                        if j + 2 < nt:
                            issue_x_load(so + order[j + 2] * 128, so, so + L, slot)
                        ns = (base + j + 1) % 2
                        gens.append((scan_front(so + order[j + 1] * 128, ns, d, ns, False), 0, ns))
                    run_pipe(gens)
            else:
                doneF, doneB1, doneB2 = {}, {}, {}

                def seqF(hold, q, so=so, nt=nt):
                    for j in range(q, nt, 2):
                        while j >= 3 and not (doneB1.get(j - 3) and doneB2.get(j - 3)):
                            yield
                        hold["par"] = j % 3
                        yield
                        t = so + j * 128
                        yield from scan_front(t, j % 3, 0, j % 3, True, q)
                        yield from scan_back(t, j % 3, 0, j % 3, True, "a", j % 3, q)
                        doneF[j] = True

                def seqB(hold, part, done, so=so, nt=nt):
                    for j in range(nt):
                        while not doneF.get(j):
                            yield
                        hold["par"] = j % 3
                        yield
                        yield from scan_back(so + j * 128, j % 2, 0, j % 3, True, part, j % 3)
                        done[j] = True

                hF0, hF1, h1, h2 = {"par": 0, "fset": 0}, {"par": 0, "fset": 1}, {"par": 0}, {"par": 0}
                gb1 = (seqB(h1, "b1", doneB1), 2, h1)
                run_pipe([gb1, (seqB(h2, "b2", doneB2), 3, h2), (seqF(hF0, 0), 0, hF0), gb1, (seqF(hF1, 1), 1, hF1)])
            tile_ctr["i"] = base + nt

    def rsqrt_small(dst, src, eps, key):
        P.act(lambda e: e.activation(out=dst, in_=src, func=AF.Ln, bias=eps), r=[key], w=[key + "r"])
        P.act(lambda e: e.activation(out=dst, in_=dst, func=AF.Exp, scale=-0.5), r=[key + "r"], w=[key + "r"])

    def layer_norm(src, skey, g_ap, b_ap, gkey, dst, dkey, stt, stkey):
        P.dve(lambda e: e.bn_stats(stt[:, 0:6], src[:, 0:512]), r=[skey], w=[stkey])
        P.dve(lambda e: e.bn_stats(stt[:, 6:12], src[:, 512:1024]), r=[skey], w=[stkey])
        P.dve(lambda e: e.bn_aggr(stt[:, 12:14], stt[:, 0:12].rearrange("p (a b) -> p a b", b=6)), r=[stkey], w=[stkey])
        rsqrt_small(stt[:, 14:15], stt[:, 13:14], LN_EPS, stkey)
        P.dve(lambda e: e.scalar_tensor_tensor(out=src, in0=src, scalar=stt[:, 12:13], in1=g_ap, op0=ALU.subtract, op1=ALU.mult),
              r=[skey, stkey, gkey], w=[skey])
        P.dve(lambda e: e.scalar_tensor_tensor(out=dst, in0=src, scalar=stt[:, 14:15], in1=b_ap, op0=ALU.mult, op1=ALU.add),
              r=[skey, stkey + "r", gkey], w=[dkey])

    def transpose_f32(src, skey, dst, dkey, identity, idk, c0=0):
        for half in range(2):
            pt, pk = pb()
            for j in range(4):
                kc = half * 4 + j
                P.pe(lambda e, kc=kc, j=j, pt=pt: e.transpose(pt[:, j * 128:(j + 1) * 128], src[:, kc * 128:(kc + 1) * 128], identity),
                     r=[skey, idk], w=pk)
            sv = pt.rearrange("p (c n) -> p c n", n=128)
            if half == 0:
                P.dve(lambda e, sv=sv: e.tensor_copy(dst[:, 0:4, c0:c0 + 128], sv), r=pk, w=[dkey + "h0"])
            else:
                P.act(lambda e, sv=sv: e.copy(dst[:, 4:8, c0:c0 + 128], sv), r=pk, w=[dkey + "h1"])

    P.parity_keys = set()
    P.fset_keys = set()
    x1sc = x2sc
    x2sc2 = nc.dram_tensor("x2sc2", [LT, D], F32).ap()
    P.barrier()
    P.sbuf_ptr = base_ptr
    NM = MT // 128
    KA = 4
    wZG = P.alloc("wZG", [128, 8, 2048], BF16)
    wO = P.alloc("wO", [128, 16, 1024], BF16)
    prmB = P.alloc("prmB", [128, 4 * 1024], F32)
    identB = P.alloc("identB", [128, 128], F32)
    identBb = P.alloc("identBb", [128, 128], BF16)
    markB = P.sbuf_ptr
    stg = {"i": 0, "t": [P.alloc(f"stgB{i}", [128, 2048], F32) for i in range(6)]}
    P.dma(prmB[:], prmB_d[:, 0:4 * 1024], "G:B", w=["prmB"])
    P.dma(identB[:], prmA_d[:, C_ID:C_ID + 128], "G:B", w=["identB"])
    P.dve(lambda e: e.tensor_copy(identBb[:], identB[:]), r=["identB"], w=["identBb"])
    load_weight(wZG, w_in_d, 8, 0, 1024, 0, stg, "wZG")
    load_weight(wZG, w_in_d, 8, 4640, 1024, 1024, stg, "wZG")
    load_weight(wO, w_out_d, 16, 0, 1024, 0, stg, "wO")
    P.barrier()
    P.sbuf_ptr = markB
    BA = []
    for k in range(KA):
        BA.append(dict(
            xtB=P.alloc("xtB", [128, 1024], F32), xTb=P.alloc("xTb", [128, 8, 128], BF16), szg=P.alloc("szg", [128, 2048], BF16),
            yfb=P.alloc("yfb", [128, 1024], F32), ybb=P.alloc("ybb", [128, 1024], F32), ofb=P.alloc("ofb", [128, 1024], F32),
            obb=P.alloc("obb", [128, 1024], F32), mix=P.alloc("mix", [128, 2048], BF16), st2=P.alloc("st2", [128, 32], F32)))
        BA[-1]["mixT"] = BA[-1]["obb"][:].bitcast(BF16).rearrange("p (c n) -> p c n", n=128)
        BA[-1]["junk"] = BA[-1]["xTb"][:].rearrange("p c n -> p (c n)")
    G_SSD, G_GLA, G_L1G, G_L1B = [prmB[:, i * 1024:(i + 1) * 1024] for i in range(4)]
    P.shared_keys = {"wZG", "wO", "prmB", "identB", "identBb"}
    pstate["groups"] = [(0, 2), (2, 2), (4, 2), (6, 2)]
    pstate["gi"] = [0, 0, 0, 0]

    def b2a_tile(tok0, k):
        B = BA[k]
        xtt, xTb, szg, yfb, ybb, ofb, obb, mix, mixT, junk, st2 = (B[n] for n in
            ("xtB", "xTb", "szg", "yfb", "ybb", "ofb", "obb", "mix", "mixT", "junk", "st2"))
        P.dma(xtt[:], x_d[tok0:tok0 + 128, :], "xtB", w=["xtB"])
        P.dma(yfb[:], ysc[0, tok0:tok0 + 128, :], "yfb", w=["yfb"])
        P.dma(ybb[:], ysc[1, tok0:tok0 + 128, :], "ybb", w=["ybb"])
        P.dma(ofb[:], osc[0, tok0:tok0 + 128, :], "ofb", w=["ofb"])
        P.dma(obb[:], osc[1, tok0:tok0 + 128, :], "obb", w=["obb", "mixT0", "mixT1"])
        yield
        transpose_f32(xtt, "xtB", xTb, "xTb", identB[:], "identB")
        yield
        for blk in range(4):
            pt, pk = pb()
            for kc in range(8):
                P.pe(lambda e, kc=kc, blk=blk, pt=pt: e.matmul(pt, xTb[:, kc, :], wZG[:, kc, blk * 512:(blk + 1) * 512],
                                                              start=(kc == 0), stop=(kc == 7)), r=["xTbh0", "xTbh1", "wZG"], w=pk)
            P.act(lambda e, blk=blk, pt=pt: e.activation(out=szg[:, blk * 512:(blk + 1) * 512], in_=pt, func=AF.Silu), r=pk, w=[f"szg{blk}"])
            yield
        P.pool(lambda e: e.memset(st2[:, 0:8], 0.0), w=["st2a", "st2c"])
        P.dve(lambda e: e.tensor_tensor(out=yfb[:], in0=yfb[:], in1=ybb[:], op=ALU.add), r=["yfb", "ybb"], w=["yfb"])
        P.dve(lambda e: e.tensor_tensor(out=ybb[:], in0=yfb[:], in1=szg[:, 0:1024], op=ALU.mult), r=["yfb", "szg0", "szg1"], w=["ybb"])
        for g in range(2):
            P.act(lambda e, g=g: e.activation(out=junk[:, 0:512], in_=ybb[:, g * 512:(g + 1) * 512], func=AF.Square, accum_out=st2[:, g:g + 1]),
                  r=["ybb"], w=["xTbh0", "xTbh1", "st2a"])
        yield
        P.dve(lambda e: e.tensor_scalar(out=st2[:, 8:10], in0=st2[:, 0:2], scalar1=1.0 / 512, scalar2=None, op0=ALU.mult),
              r=["st2a"], w=["st2b"])
        rsqrt_small(st2[:, 8:10], st2[:, 8:10], RMS_EPS, "st2b")
        for g in range(2):
            P.dve(lambda e, g=g: e.scalar_tensor_tensor(out=mix[:, g * 512:(g + 1) * 512], in0=ybb[:, g * 512:(g + 1) * 512],
                                                        scalar=st2[:, 8 + g:9 + g], in1=G_SSD[:, g * 512:(g + 1) * 512],
                                                        op0=ALU.mult, op1=ALU.mult), r=["ybb", "st2br", "prmB"], w=["mixa"])
        yield
        P.pool(lambda e: e.tensor_tensor(out=ofb[:], in0=ofb[:], in1=obb[:], op=ALU.add), r=["ofb", "obb"], w=["ofb"])
        for h in range(4):
            P.act(lambda e, h=h: e.activation(out=junk[:, 0:256], in_=ofb[:, h * 256:(h + 1) * 256], func=AF.Square, accum_out=st2[:, 2 + h:3 + h]),
                  r=["ofb"], w=["xTbh0", "xTbh1", "st2c"])
        yield
        P.dve(lambda e: e.tensor_scalar(out=st2[:, 12:16], in0=st2[:, 2:6], scalar1=1.0 / 256, scalar2=None, op0=ALU.mult),
              r=["st2c"], w=["st2d"])
        rsqrt_small(st2[:, 12:16], st2[:, 12:16], RMS_EPS, "st2d")
        for h in range(4):
            P.dve(lambda e, h=h: e.scalar_tensor_tensor(out=obb[:, h * 256:(h + 1) * 256], in0=ofb[:, h * 256:(h + 1) * 256],
                                                        scalar=st2[:, 12 + h:13 + h], in1=G_GLA[:, h * 256:(h + 1) * 256],
                                                        op0=ALU.mult, op1=ALU.mult), r=["ofb", "st2dr", "prmB"], w=["obb"])
        P.pool(lambda e: e.tensor_tensor(out=mix[:, 1024:2048], in0=obb[:], in1=szg[:, 1024:2048], op=ALU.mult),
               r=["obb", "szg2", "szg3"], w=["mixb"])
        yield
        for half in range(2):
            pt, pk = pb()
            ptb = pt.bitcast(BF16)
            for j in range(8):
                c = half * 8 + j
                P.pe(lambda e, c=c, j=j, ptb=ptb: e.transpose(ptb[:, j * 128:(j + 1) * 128], mix[:, c * 128:(c + 1) * 128], identBb[:]),
                     r=["mixa" if c < 8 else "mixb", "identBb"], w=pk)
            sv = ptb.rearrange("p (c n) -> p c n", n=128)
            if half == 0:
                P.act(lambda e, sv=sv: e.copy(mixT[:, 0:8, :], sv), r=pk + ["mixb"], w=["mixT0", "obb"])
            else:
                P.dve(lambda e, sv=sv: e.tensor_copy(mixT[:, 8:16, :], sv), r=pk + ["mixb"], w=["mixT1", "obb"])
            yield
        po1, po1k = pb(2)
        for half in range(2):
            for c in range(16):
                P.pe(lambda e, c=c, half=half: e.matmul(po1[:, half * 512:(half + 1) * 512], mixT[:, c, :], wO[:, c, half * 512:(half + 1) * 512],
                                                       start=(c == 0), stop=(c == 15)), r=[f"mixT{c // 8}", "wO"], w=[po1k[half]])
            yield
        P.dve(lambda e: e.scalar_tensor_tensor(out=yfb[:], in0=xtt[:], scalar=ALPHA, in1=po1, op0=ALU.mult, op1=ALU.add),
              r=["xtB"] + po1k, w=["yfb"])
        yield
        layer_norm(yfb[:], "yfb", G_L1G, G_L1B, "prmB", ybb[:], "ybb", st2[:, 16:32], "st2e")
        P.dma(x1sc[tok0:tok0 + 128, :], ybb[:], "ybbs", r=["ybb"])

    tiles = []
    for si in range(NS):
        for ti in range(seq_lens[si] // 128):
            tiles.append((seq_offs[si] + ti * 128, si))
    run_streams(tiles, lambda it, k: b2a_tile(it[0], k), KA, 4)

    P.shared_keys = set()
    P.barrier()
    P.sbuf_ptr = base_ptr
    KB = 8
    wQ = P.alloc("wQ", [128, 8, 1024], BF16)
    wMO = P.alloc("wMO", [128, 8, 1024], BF16)
    prmB2 = P.alloc("prmB2", [128, 2 * 1024], F32)
    identQ = P.alloc("identB2", [128, 128], F32)
    identQb = P.alloc("identBb2", [128, 128], BF16)
    KT_all = [P.alloc(f"KT{i}", [128, 8, MT], BF16) for i in range(NS)]
    Vm_all = [P.alloc(f"Vm{i}", [128, NM, 1024], BF16) for i in range(NS)]
    markB2 = P.sbuf_ptr
    stg = {"i": 0, "t": [P.alloc(f"stgB{i}", [128, 2048], F32) for i in range(6)]}
    wK = P.alloc("wK", [128, 8, 1024], BF16)
    wV = P.alloc("wV", [128, 8, 1024], BF16)
    memT = P.alloc("memT", [128, 8, MT], BF16)
    P.dma(prmB2[:], prmB_d[:, 4 * 1024:6 * 1024], "G:B2", w=["prmB2"])
    P.dma(identQ[:], prmA_d[:, C_ID:C_ID + 128], "G:B2", w=["identB"])
    P.dve(lambda e: e.tensor_copy(identQb[:], identQ[:]), r=["identB"], w=["identBb"])
    load_weight(wQ, w_mq_d, 8, 0, 1024, 0, stg, "wQ")
    load_weight(wMO, w_mo_d, 8, 0, 1024, 0, stg, "wMO")
    load_weight(wK, w_mk_d, 8, 0, 1024, 0, stg, "wK")
    load_weight(wV, w_mv_d, 8, 0, 1024, 0, stg, "wV")
    for si in range(NS):
        for mc in range(NM):
            s_ = stg["i"] % 3
            stg["i"] += 1
            sk = f"stg{s_}"
            stt = stg["t"][s_]
            r0 = si * MT + mc * 128
            P.dma(stt[:, 0:1024], mem_d[r0:r0 + 128, :], sk, w=[sk])
            for half in range(2):
                pt, pk = pb()
                for j in range(4):
                    kc = half * 4 + j
                    P.pe(lambda e, kc=kc, j=j, pt=pt, stt=stt: e.transpose(pt[:, j * 128:(j + 1) * 128], stt[:, kc * 128:(kc + 1) * 128], identQ[:]),
                         r=[sk, "identB"], w=pk)
                P.dve(lambda e, pt=pt, half=half, mc=mc: e.tensor_copy(memT[:, half * 4:half * 4 + 4, mc * 128:(mc + 1) * 128],
                                                                      pt.rearrange("p (c n) -> p c n", n=128)), r=pk, w=["memT"])
        for dc in range(8):
            pt, pk = pb()
            for kc in range(8):
                P.pe(lambda e, dc=dc, kc=kc, pt=pt: e.matmul(pt[:, 0:MT], wK[:, kc, dc * 128:(dc + 1) * 128], memT[:, kc, :],
                                                            start=(kc == 0), stop=(kc == 7)), r=["wK", "memT"], w=pk)
            P.act(lambda e, dc=dc, pt=pt, si=si: e.copy(KT_all[si][:, dc, :], pt[:, 0:MT]), r=pk, w=[f"KT{si}"])
        for mc in range(NM):
            for half in range(2):
                pt, pk = pb()
                for kc in range(8):
                    P.pe(lambda e, mc=mc, kc=kc, half=half, pt=pt: e.matmul(pt, memT[:, kc, mc * 128:(mc + 1) * 128],
                                                                           wV[:, kc, half * 512:(half + 1) * 512],
                                                                           start=(kc == 0), stop=(kc == 7)), r=["wV", "memT"], w=pk)
                P.dve(lambda e, mc=mc, half=half, pt=pt, si=si: e.tensor_copy(Vm_all[si][:, mc, half * 512:(half + 1) * 512], pt),
                      r=pk, w=[f"Vm{si}"])
    P.barrier()
    P.sbuf_ptr = markB2
    BB = []
    for k in range(KB):
        BB.append(dict(x1=P.alloc("x1", [128, 1024], F32), x1T=P.alloc("x1T", [128, 8, 128], BF16), qT=P.alloc("qT", [128, 8, 128], BF16),
                       Pm=P.alloc("Pm", [128, 4, 256], BF16), PT=P.alloc("PT", [128, 8, 128], BF16), r2=P.alloc("r2", [128, 1024], F32),
                       st2=P.alloc("st2", [128, 32], F32)))
    G_L2G, G_L2B = prmB2[:, 0:1024], prmB2[:, 1024:2048]
    P.shared_keys = {"wQ", "wMO", "prmB2", "identB", "identBb"} | {f"KT{i}" for i in range(NS)} | {f"Vm{i}" for i in range(NS)}
    pstate["groups"] = [(i, 1) for i in range(8)]
    pstate["gi"] = [0] * 8

    def b2b_tile(tok0, si, k):
        B = BB[k]
        x1, x1T, qT, Pm, PT, r2, st2 = (B[n] for n in ("x1", "x1T", "qT", "Pm", "PT", "r2", "st2"))
        oT = qT
        P.dma(x1[:], x1sc[tok0:tok0 + 128, :], "x1", w=["x1"])
        yield
        transpose_f32(x1, "x1", x1T, "x1T", identQ[:], "identB")
        yield
        for qh in range(2):
            pqq, pqqk = pb()
            for d4 in range(4):
                dc = qh * 4 + d4
                for kc in range(8):
                    P.pe(lambda e, dc=dc, d4=d4, kc=kc, pqq=pqq: e.matmul(pqq[:, d4 * 128:(d4 + 1) * 128], wQ[:, kc, dc * 128:(dc + 1) * 128],
                                                                         x1T[:, kc, :], start=(kc == 0), stop=(kc == 7)),
                         r=["wQ", "x1Th0", "x1Th1"], w=pqqk)
                if d4 % 2 == 1:
                    yield
            P.act(lambda e, qh=qh, pqq=pqq: e.mul(qT[:, qh * 4:(qh + 1) * 4, :].rearrange("p c n -> p (c n)"), pqq, 256.0 ** -0.5),
                  r=pqqk, w=[f"qT{qh}"])
            yield
        P.pool(lambda e: e.memset(st2[:, 8:12], 0.0), w=["st2h0", "st2h1"])
        for hp in range(2):
            pS, pSk = pb()
            for hh in range(2):
                h = hp * 2 + hh
                for j in range(2):
                    P.pe(lambda e, h=h, hh=hh, j=j, pS=pS: e.matmul(pS[:, hh * 256:(hh + 1) * 256], qT[:, 2 * h + j, :], KT_all[si][:, 2 * h + j, :],
                                                                   start=(j == 0), stop=(j == 1)), r=[f"qT{hp}", f"KT{si}"], w=pSk)
            P.dve(lambda e, hp=hp, pS=pS: e.tensor_reduce(out=st2[:, 2 * hp:2 * hp + 2], in_=pS.rearrange("p (h m) -> p h m", h=2), axis=AX.X, op=ALU.max),
                  r=pSk, w=[f"st2f{hp}"])
            P.dve(lambda e, hp=hp: e.tensor_scalar(out=st2[:, 4 + 2 * hp:6 + 2 * hp], in0=st2[:, 2 * hp:2 * hp + 2], scalar1=-1.0, scalar2=None, op0=ALU.mult),
                  r=[f"st2f{hp}"], w=[f"st2g{hp}"])
            yield
            for hh in range(2):
                h = hp * 2 + hh
                P.act(lambda e, h=h, hh=hh, pS=pS: e.activation(out=Pm[:, h, :], in_=pS[:, hh * 256:(hh + 1) * 256], func=AF.Exp, bias=st2[:, 4 + h:5 + h],
                                                               accum_out=st2[:, 8 + h:9 + h]), r=pSk + [f"st2g{hp}"], w=[f"Pm{hp}", f"st2h{hp}"])
            yield
        P.dve(lambda e: e.reciprocal(st2[:, 12:16], st2[:, 8:12]), r=["st2h0", "st2h1"], w=["st2i"])
        P.dve(lambda e: e.tensor_tensor(out=Pm[:], in0=Pm[:], in1=st2[:, 12:16].unsqueeze(2).to_broadcast([128, 4, 256]), op=ALU.mult),
              r=["Pm0", "Pm1", "st2i"], w=["Pm0", "Pm1"])
        yield
        pt, pk = pb()
        ptb = pt.bitcast(BF16)
        for h in range(4):
            for j in range(2):
                c = h * 2 + j
                P.pe(lambda e, c=c, h=h, j=j, ptb=ptb: e.transpose(ptb[:, c * 128:(c + 1) * 128], Pm[:, h, j * 128:(j + 1) * 128], identQb[:]),
                     r=["Pm0", "Pm1", "identBb"], w=pk)
        P.act(lambda e, ptb=ptb: e.copy(PT[:].rearrange("p c n -> p (c n)"), ptb), r=pk, w=["PT"])
        yield
        for oh in range(2):
            poT, poTk = pb()
            for c4 in range(4):
                c = oh * 4 + c4
                h, j2 = c // 2, c % 2
                for j in range(2):
                    P.pe(lambda e, c4=c4, h=h, j=j, j2=j2, poT=poT: e.matmul(poT[:, c4 * 128:(c4 + 1) * 128],
                                                                            Vm_all[si][:, j, h * 256 + j2 * 128:h * 256 + (j2 + 1) * 128], PT[:, h * 2 + j, :],
                                                                            start=(j == 0), stop=(j == 1)), r=[f"Vm{si}", "PT"], w=poTk)
            P.act(lambda e, oh=oh, poT=poT: e.copy(oT[:, oh * 4:(oh + 1) * 4, :].rearrange("p c n -> p (c n)"), poT), r=poTk, w=["qT0", "qT1", f"oT{oh}"])
            yield
        for half in range(2):
            po2, po2k = pb()
            for c in range(8):
                P.pe(lambda e, c=c, half=half, po2=po2: e.matmul(po2, oT[:, c, :], wMO[:, c, half * 512:(half + 1) * 512],
                                                                start=(c == 0), stop=(c == 7)), r=["oT0", "oT1", "qT0", "qT1", "wMO"], w=po2k)
            P.dve(lambda e, half=half, po2=po2: e.scalar_tensor_tensor(out=r2[:, half * 512:(half + 1) * 512], in0=x1[:, half * 512:(half + 1) * 512],
                                                                      scalar=ALPHA, in1=po2, op0=ALU.mult, op1=ALU.add),
                  r=["x1"] + po2k, w=["r2"])
            yield
        layer_norm(r2[:], "r2", G_L2G, G_L2B, "prmB2", x1[:], "x1", st2[:, 16:32], "st2j")
        P.dma(x2sc2[tok0:tok0 + 128, :], x1[:], "x1s", r=["x1"])

    run_streams(tiles, lambda it, k: b2b_tile(it[0], it[1], k), KB, 3)
    P.shared_keys = set()
    x2sc = x2sc2

    P.barrier()
    P.sbuf_ptr = base_ptr
    NB = 2
    w1 = P.alloc("w1", [128, 8, 4096], BF16)
    w2 = P.alloc("w2", [128, 32, 1024], BF16)
    prmC = P.alloc("prmC", [128, 2048], F32)
    identC = P.alloc("identC", [128, 128], F32)
    markC = P.sbuf_ptr
    stg = {"i": 0, "t": [P.alloc(f"stgC{i}", [128, 2048], F32) for i in range(6)]}
    P.dma(prmC[:], prmB_d[:, 6 * 1024:8 * 1024], "G:C", w=["prmC"])
    P.dma(identC[:], prmA_d[:, C_ID:C_ID + 128], "G:C", w=["identC"])
    load_weight(w1, w_ff1_d, 8, 0, 4096, 0, stg, "w1")
    load_weight(w2, w_ff2_d, 32, 0, 1024, 0, stg, "w2")
    P.barrier()
    P.sbuf_ptr = markC
    x2b = [P.alloc(f"x2b{i}", [128, NB, 1024], F32) for i in range(2)]
    x2T = P.alloc("x2T", [128, 8, NB * 128], BF16)
    h1T = P.alloc("h1T", [128, 32, NB * 128], BF16)
    rtmp = [P.alloc(f"rtmp{i}", [128, 2, NB * 128], BF16) for i in range(2)]
    r3 = P.alloc("r3", [128, 1024], F32)
    yo3 = [P.alloc(f"yo3{i}", [128, 1024], F32) for i in range(2)]
    st3 = P.alloc("st3", [128, 16], F32)

    blocks = []
    for si in range(NS):
        nt = seq_lens[si] // 128
        for b in range(0, nt, NB):
            blocks.append((seq_offs[si] + b * 128, min(NB, nt - b)))

    def b3_load(tok0, nb, slot):
        P.dma(x2b[slot][:, 0:nb, :], x2sc[tok0:tok0 + nb * 128, :].rearrange("(t p) d -> p t d", p=128), f"x2b{slot}", w=[f"x2b{slot}"])

    octr = {"i": 0}
    b3_load(blocks[0][0], blocks[0][1], 0)
    for j, (tok0, nb) in enumerate(blocks):
        slot = j % 2
        if j + 1 < len(blocks):
            b3_load(blocks[j + 1][0], blocks[j + 1][1], (j + 1) % 2)
        xk = f"x2b{slot}"
        N = nb * 128
        for t in range(nb):
            transpose_f32(x2b[slot][:, t, :], xk, x2T, "x2T", identC[:], "identC", c0=t * 128)
        for fp in range(16):
            pt, pk = pb()
            for q in range(2):
                fc = fp * 2 + q
                for kc in range(8):
                    P.pe(lambda e, fc=fc, kc=kc, q=q, pt=pt, N=N: e.matmul(pt[:, q * N:(q + 1) * N], w1[:, kc, fc * 128:(fc + 1) * 128], x2T[:, kc, 0:N],
                                                                          start=(kc == 0), stop=(kc == 7)), r=["w1", "x2Th0", "x2Th1"], w=pk)
            rt = rtmp[fp % 2]
            rk = f"rtmp{fp % 2}"
            P.act(lambda e, rt=rt, pt=pt, N=N: e.activation(out=rt[:, :, 0:N], in_=pt[:, 0:2 * N].rearrange("p (q n) -> p q n", q=2), func=AF.Relu),
                  r=pk, w=[rk])
            P.pool(lambda e, rt=rt, fp=fp, N=N: e.tensor_tensor(out=h1T[:, fp * 2:fp * 2 + 2, 0:N], in0=rt[:, :, 0:N], in1=rt[:, :, 0:N], op=ALU.mult),
                   r=[rk], w=[f"h1T{fp // 4}"])
        for t in range(nb):
            p3, p3k = pb(2)
            for half in range(2):
                for fc in range(32):
                    P.pe(lambda e, fc=fc, half=half, t=t, p3=p3: e.matmul(p3[:, half * 512:(half + 1) * 512], h1T[:, fc, t * 128:(t + 1) * 128],
                                                                         w2[:, fc, half * 512:(half + 1) * 512], start=(fc == 0), stop=(fc == 31)),
                         r=[f"h1T{fc // 8}", "w2"], w=[p3k[half]])
            P.dve(lambda e, t=t, p3=p3, slot=slot: e.scalar_tensor_tensor(out=r3[:], in0=x2b[slot][:, t, :], scalar=ALPHA, in1=p3, op0=ALU.mult, op1=ALU.add),
                  r=[xk] + p3k, w=["r3"])
            os_ = octr["i"] % 2
            octr["i"] += 1
            layer_norm(r3[:], "r3", prmC[:, 0:1024], prmC[:, 1024:2048], "prmC", yo3[os_][:], f"yo3{os_}", st3[:], "st3")
            P.dma(y_d[tok0 + t * 128:tok0 + (t + 1) * 128, :], yo3[os_][:], f"yo3{os_}s", r=[f"yo3{os_}"])

    finals = [k for k in P.dma_last if k.startswith("yo3") and k.endswith("s")]
    P.emit(final_wait_keys=finals)
    st.close()
    return nc, P


_CACHE = {}


def _prep_shared(inp):
    g = lambda k: np.asarray(inp[k], dtype=np.float32)[0]
    prmA = np.zeros((128, NPA), np.float32)
    prmA[:, 0:1280] = host_consts()
    cw = g("conv_w")
    prmA[:, C_CW:C_CW + 60] = cw.reshape(5, 12, 128).transpose(2, 1, 0).reshape(128, 60)
    prmA[:, C_CB:C_CB + 12] = g("conv_b").reshape(12, 128).T
    prmA[:, C_DTB:C_DTB + 16] = g("dt_bias_f")[None, :]
    prmA[:, C_DTB + 16:C_DTB + 32] = g("dt_bias_b")[None, :]
    prmA[:, C_AL:C_AL + 16] = g("a_log_f")[None, :]
    prmA[:, C_AL + 16:C_AL + 32] = g("a_log_b")[None, :]
    prmA[:, C_DS:C_DS + 16] = g("d_skip")[None, :]
    rowp = np.zeros((17, 2560), np.float32)
    rowp[0:16, 0:512] = g("w_gk_f")
    rowp[16, 0:512] = g("b_gk_f")
    rowp[0:16, 512:1024] = g("w_gk_b")
    rowp[16, 512:1024] = g("b_gk_b")
    rowp[0, 1024:2560] = g("conv_b")
    prmB = np.zeros((128, 8 * 1024), np.float32)
    prmB[:, 0:1024] = g("ssd_norm_g")[None, :]
    prmB[:, 1024:2048] = np.tile(g("gla_norm_g"), 4)[None, :]
    for i, k in enumerate(["ln1_g", "ln1_b", "ln2_g", "ln2_b", "ln3_g", "ln3_b"]):
        prmB[:, (2 + i) * 1024:(3 + i) * 1024] = g(k)[None, :]
    shared = {"prmA": prmA, "rowp": rowp, "prmB": prmB}
    for k in ["w_in", "w_out", "w_mq", "w_mk", "w_mv", "w_mo", "w_ff1", "w_ff2"]:
        shared[k] = np.ascontiguousarray(g(k))
    return shared


def kernel(**inp):
    xp = np.asarray(inp["x_prompt"], dtype=np.float32)
    xs = np.asarray(inp["x_sample"], dtype=np.float32)
    mp = np.asarray(inp["mem_prompt"], dtype=np.float32)
    ms = np.asarray(inp["mem_sample"], dtype=np.float32)
    n = 8
    Lp, Ls = xp.shape[1], xs.shape[1]
    key = (Lp, Ls)
    if key not in _CACHE:
        _CACHE[key] = build([Lp, Ls], mem_tokens=mp.shape[1])[0]
    nc = _CACHE[key]
    shared = _prep_shared(inp)
    in_maps = []
    for c in range(n):
        m = dict(shared)
        m["x"] = np.concatenate([xp[c], xs[c]], axis=0)
        m["mem"] = np.concatenate([mp[c], ms[c]], axis=0)
        in_maps.append(m)
    res = run_bass_kernel_spmd(nc, in_maps, core_ids=list(range(n)))
    yp = np.stack([res.results[c]["y"][:Lp] for c in range(n)], axis=0)
    ys = np.stack([res.results[c]["y"][Lp:] for c in range(n)], axis=0)
    return (yp.astype(np.float32), ys.astype(np.float32))
```

```python
import contextlib
import numpy as np
import concourse.bass as bass
import concourse.mybir as mybir
from concourse.bass_utils import run_bass_kernel_spmd

F32 = mybir.dt.float32
BF16 = mybir.dt.bfloat16
AF = mybir.ActivationFunctionType
ALU = mybir.AluOpType
AX = mybir.AxisListType

ENGS = ("pe", "act", "dve", "pool", "sp")
PIPE = {"A": True, "B": True}
SAME_ENGINE_SYNC = True

D = 1024
IN_PROJ = 5680
ALPHA = 2.0 ** 0.25
LN_EPS = 1e-5
RMS_EPS = 1e-5


class _Op:
    __slots__ = ("eng", "fn", "waits", "idx", "dma", "rank")


class Prog:
    def __init__(self, nc):
        self.nc = nc
        self.streams = {e: [] for e in ENGS}
        self.last_w = {}
        self.readers = {}
        self.seen = {e: {} for e in ENGS}
        self.pending = {e: [] for e in ENGS}
        self.dma_count = {}
        self.dma_last = {}
        self.sbuf_ptr = nc.sbuf_base
        self.sbuf_max = 0
        self.n_t = 0
        self.par = 0
        self.parity_keys = set()
        self.stream = None
        self.shared_keys = set()
        self.fset = 0
        self.fset_keys = set()

    def alloc(self, name, shape, dtype):
        nbytes = int(np.prod(shape[1:])) * mybir.dt.size(dtype)
        off = (self.sbuf_ptr + 31) // 32 * 32
        self.sbuf_ptr = off + nbytes
        self.sbuf_max = max(self.sbuf_max, self.sbuf_ptr)
        assert self.sbuf_ptr <= self.nc.sbuf_top, (name, self.sbuf_ptr)
        self.n_t += 1
        return self.nc.alloc_sbuf_tensor_at(f"{name}_{self.n_t}", list(shape), dtype, offset=off)

    def _dep(self, op, d):
        if d is None:
            return
        e = op.eng
        if d.dma is not None:
            key, val = d.dma
            if key.startswith("G:"):
                val = self.dma_count[key]
            if self.seen[e].get(("dma", key), 0) >= val:
                return
            op.waits.append((d, val))
            self.seen[e][("dma", key)] = val
        else:
            if d.eng == e:
                if e in ("pe", "sp") or not SAME_ENGINE_SYNC:
                    return
            if self.seen[e].get(d.eng, -1) >= d.idx:
                return
            op.waits.append((d, None))
            self.seen[e][d.eng] = d.idx

    def op(self, eng, fn, r=(), w=(), dma=None):
        if self.stream is not None:
            sfx = "#" + str(self.stream)
            sh = self.shared_keys
            r = [k if (k in sh or k.startswith("pb")) else k + sfx for k in r]
            w = [k if (k in sh or k.startswith("pb")) else k + sfx for k in w]
            if dma is not None:
                dma = dma + sfx
        if any(k.startswith("pb") for k in r):
            w = list(w) + [k for k in r if k.startswith("pb")]
            r = [k for k in r if not k.startswith("pb")]
        fk_ = self.fset_keys
        if fk_:
            r = [k + "$" + str(self.fset) if k in fk_ else k for k in r]
            w = [k + "$" + str(self.fset) if k in fk_ else k for k in w]
        pk_ = self.parity_keys
        if pk_:
            r = [k + "@" + str(self.par) if k in pk_ else k for k in r]
            w = [k + "@" + str(self.par) if k in pk_ else k for k in w]
        o = _Op()
        o.eng = eng
        o.fn = fn
        o.waits = []
        o.idx = len(self.streams[eng])
        o.rank = None
        if dma is not None:
            c = self.dma_count.get(dma, 0) + 16
            self.dma_count[dma] = c
            o.dma = (dma, c)
            self.dma_last[dma] = o
        else:
            o.dma = None
        for d in self.pending[eng]:
            self._dep(o, d)
        self.pending[eng] = []
        for k in r:
            self._dep(o, self.last_w.get(k))
        for k in w:
            self._dep(o, self.last_w.get(k))
            for rd in self.readers.get(k, ()):
                if rd is not o:
                    self._dep(o, rd)
        for k in r:
            self.readers.setdefault(k, []).append(o)
        for k in w:
            self.last_w[k] = o
            self.readers[k] = []
        self.streams[eng].append(o)
        return o

    def barrier(self):
        lasts = []
        for e in ENGS:
            if e != "sp" and self.streams[e]:
                lasts.append(self.streams[e][-1])
        lasts += list(self.dma_last.values())
        for e in ENGS:
            self.pending[e] = list(lasts)
        self.last_w = {}
        self.readers = {}

    def pe(self, fn, r=(), w=()):
        return self.op("pe", fn, r, w)

    def act(self, fn, r=(), w=()):
        return self.op("act", fn, r, w)

    def dve(self, fn, r=(), w=()):
        return self.op("dve", fn, r, w)

    def pool(self, fn, r=(), w=()):
        return self.op("pool", fn, r, w)

    def dma(self, out, in_, key, r=(), w=(), **kw):
        return self.op("sp", lambda e: e.dma_start(out=out, in_=in_, **kw), r, w, dma=key)

    def emit(self, final_wait_keys=()):
        nc = self.nc
        for e in ENGS:
            for o in self.streams[e]:
                for d, _ in o.waits:
                    if d.dma is None:
                        d.rank = 0
        finals = [self.dma_last[k] for k in final_wait_keys]
        for e in ENGS:
            if e == "sp":
                continue
            n = 0
            for o in self.streams[e]:
                if o.rank is not None:
                    n += 1
                    o.rank = n
        with contextlib.ExitStack() as st:
            esem = {e: st.enter_context(nc.semaphore(f"s_{e}")) for e in ENGS if e != "sp"}
            dsem = {k: st.enter_context(nc.semaphore(f"d_{i}")) for i, k in enumerate(self.dma_count)}
            block = st.enter_context(nc.Block())

            def run(eng_name):
                def body(eng):
                    for o in self.streams[eng_name]:
                        for d, v in o.waits:
                            if d.dma is not None:
                                eng.wait_ge(dsem[d.dma[0]], v)
                            else:
                                eng.wait_ge(esem[d.eng], d.rank)
                        ins = o.fn(eng)
                        if o.dma is not None:
                            ins.then_inc(dsem[o.dma[0]], 16)
                        elif o.rank is not None:
                            ins.then_inc(esem[o.eng], 1)
                    if eng_name == "sp":
                        for d in finals:
                            eng.wait_ge(dsem[d.dma[0]], d.dma[1])
                return body

            block.tensor(run("pe"))
            block.scalar(run("act"))
            block.vector(run("dve"))
            block.gpsimd(run("pool"))
            block.sync(run("sp"))


C_ID, C_MLF, C_MLB, C_SUF, C_SUB, C_ONE, C_TRF, C_TRB, C_S16F, C_S16B = [i * 128 for i in range(10)]
C_CW = 1280
C_CB = C_CW + 60
C_DTB = C_CB + 12
C_AL = C_DTB + 32
C_DS = C_AL + 32
NPA = C_DS + 16


def host_consts():
    k = np.arange(128)
    mlf = (k[:, None] <= k[None, :]).astype(np.float32)
    mlb = (k[:, None] >= k[None, :]).astype(np.float32)
    suf = (k[:, None] > k[None, :]).astype(np.float32)
    sub = (k[:, None] < k[None, :]).astype(np.float32)
    return np.concatenate([np.eye(128, dtype=np.float32), mlf, mlb, suf, sub, np.ones((128, 128), np.float32),
                           -mlf / 16.0, -mlb / 16.0, -suf / 16.0, -sub / 16.0], axis=1)


def build(seq_lens, mem_tokens=256):
    nc = bass.Bass("TRN2", target_bir_lowering=False)
    LT = sum(seq_lens)
    NS = len(seq_lens)
    MT = mem_tokens
    x_d = nc.dram_tensor("x", [LT, D], F32, kind="ExternalInput").ap()
    mem_d = nc.dram_tensor("mem", [NS * MT, D], F32, kind="ExternalInput").ap()
    w_in_d = nc.dram_tensor("w_in", [D, IN_PROJ], F32, kind="ExternalInput").ap()
    w_out_d = nc.dram_tensor("w_out", [2048, D], F32, kind="ExternalInput").ap()
    w_mq_d = nc.dram_tensor("w_mq", [D, D], F32, kind="ExternalInput").ap()
    w_mk_d = nc.dram_tensor("w_mk", [D, D], F32, kind="ExternalInput").ap()
    w_mv_d = nc.dram_tensor("w_mv", [D, D], F32, kind="ExternalInput").ap()
    w_mo_d = nc.dram_tensor("w_mo", [D, D], F32, kind="ExternalInput").ap()
    w_ff1_d = nc.dram_tensor("w_ff1", [D, 4096], F32, kind="ExternalInput").ap()
    w_ff2_d = nc.dram_tensor("w_ff2", [4096, D], F32, kind="ExternalInput").ap()
    prmA_d = nc.dram_tensor("prmA", [128, NPA], F32, kind="ExternalInput").ap()
    rowp_d = nc.dram_tensor("rowp", [17, 2560], F32, kind="ExternalInput").ap()
    prmB_d = nc.dram_tensor("prmB", [128, 8 * 1024], F32, kind="ExternalInput").ap()
    y_d = nc.dram_tensor("y", [LT, D], F32, kind="ExternalOutput").ap()
    ysc = nc.dram_tensor("ysc", [2, LT, D], F32).ap()
    osc = nc.dram_tensor("osc", [2, LT, D], F32).ap()
    x2sc = nc.dram_tensor("x2sc", [LT, D], F32).ap()
    packsc = nc.dram_tensor("packsc", [LT // 128, 128, 4352], BF16).ap()
    dtsc = nc.dram_tensor("dtsc", [LT // 128, 128, 16], F32).ap()
    lrsc = nc.dram_tensor("lrsc", [LT // 128, 16, 128], F32).ap()

    P = Prog(nc)
    st = contextlib.ExitStack()
    psum = st.enter_context(nc.psum_tensor("psum", [128, 4096], F32))
    pstate = {"i": 0, "g": None, "gi": [0, 0, 0, 0], "phase": "A", "groups": [(0, 4), (4, 4)]}

    def pb(n=1):
        g = pstate["g"]
        if g is None:
            lo, nbk, i = 0, 8, pstate["i"]
        else:
            lo, nbk = pstate["groups"][g]
            i = pstate["gi"][g]
        if n == 2 and i % 2:
            i += 1
        if i + n > nbk:
            i = 0
        if g is None:
            pstate["i"] = (i + n) % nbk
        else:
            pstate["gi"][g] = (i + n) % nbk
        i += lo
        return psum[:, i * 512:(i + n) * 512], [f"pb{j}" for j in range(i, i + n)]

    def run_streams(items, make_gen, K, stagger):
        pend = list(items)
        active = []
        free = list(range(K))
        since = 10 ** 9
        while pend or active:
            if pend and free and since >= stagger:
                slot = free.pop(0)
                active.append((make_gen(pend.pop(0), slot), slot))
                since = 0
            for it in list(active):
                g, slot = it
                pstate["g"] = slot
                P.stream = slot
                try:
                    next(g)
                except StopIteration:
                    active.remove(it)
                    free.append(slot)
            since += 1
        pstate["g"] = None
        P.stream = None

    def run_pipe(gens):
        live = list(gens)
        if not PIPE[pstate["phase"]]:
            for g, grp, par in live:
                pstate["g"] = grp
                P.par = par
                for _ in g:
                    pass
            pstate["g"] = None
            return
        while live:
            for it in list(live):
                g, grp, par = it
                pstate["g"] = grp
                P.par = par["par"] if isinstance(par, dict) else par
                P.fset = par.get("fset", 0) if isinstance(par, dict) else 0
                try:
                    next(g)
                except StopIteration:
                    live.remove(it)
        pstate["g"] = None

    cast_rr = {"i": 0}

    def cast(out, in_, r, w):
        i = cast_rr["i"]
        cast_rr["i"] = i + 1
        if i % 3 == 0:
            P.dve(lambda e: e.tensor_copy(out, in_), r=r, w=w)
        elif i % 3 == 1:
            P.act(lambda e: e.copy(out, in_), r=r, w=w)
        else:
            P.pool(lambda e: e.tensor_copy(out, in_), r=r, w=w)

    def load_weight(dst, src, kchunks, col0, ncols, dcol0, stg, tag):
        CH = 2048
        n = 0
        for kc in range(kchunks):
            for c in range(0, ncols, CH):
                cn = min(CH, ncols - c)
                s = stg["i"] % len(stg["t"])
                stg["i"] += 1
                sk = f"stg{s}"
                stt = stg["t"][s]
                P.dma(stt[:, 0:cn], src[kc * 128:(kc + 1) * 128, col0 + c:col0 + c + cn], sk, w=[sk])
                cast(dst[:, kc, dcol0 + c:dcol0 + c + cn], stt[:, 0:cn], r=[sk], w=[tag])
                n += 1

    base_ptr = P.sbuf_ptr

    P.sbuf_ptr = base_ptr
    wS = P.alloc("wS", [128, 8, 3632], BF16)
    diagw = P.alloc("diagw", [128, 12, 5, 128], BF16)
    DI = P.alloc("DI", [128, 16, 128], BF16)
    prmA = P.alloc("prmA", [128, NPA], F32)
    wgk = P.alloc("wgk", [17, 1024], F32)
    cbrow_f = P.alloc("cbrow_f", [1, 1536], F32)
    cbrow = P.alloc("cbrow", [1, 1536], BF16)
    identb = P.alloc("identb", [128, 128], BF16)
    onesb = P.alloc("onesb", [1, 128], BF16)
    aneg = P.alloc("aneg", [128, 32], F32)
    lrT = P.alloc("lrT", [17, 128], F32)
    hT = P.alloc("hT", [128, 1024], F32)
    hTb = P.alloc("hTb", [128, 1024], BF16)
    Hs = P.alloc("Hs", [128, 4, 256], F32)
    Hsb = P.alloc("Hsb", [128, 4, 256], BF16)
    markA = P.sbuf_ptr
    stg = {"i": 0, "t": [P.alloc(f"stg{i}", [128, 2048], F32) for i in range(6)]}

    P.dma(prmA[:], prmA_d[:, :], "G:A", w=["prmA"])
    P.dma(wgk[:], rowp_d[:, 0:1024], "G:A", w=["wgk"])
    P.dma(cbrow_f[:], rowp_d[0:1, 1024:2560], "G:A", w=["cbrow_f"])
    load_weight(wS, w_in_d, 8, 1024, 3616, 0, stg, "wS")
    load_weight(wS, w_in_d, 8, 5664, 16, 3616, stg, "wS")
    ident = prmA[:, C_ID:C_ID + 128]
    ones = prmA[:, C_ONE:C_ONE + 128]
    P.dve(lambda e: e.tensor_copy(identb[:], ident), r=["prmA"], w=["identb"])
    P.dve(lambda e: e.tensor_copy(onesb[:], prmA[0:1, C_ONE:C_ONE + 128]), r=["prmA"], w=["onesb"])
    P.dve(lambda e: e.tensor_copy(cbrow[:], cbrow_f[:]), r=["cbrow_f"], w=["cbrow"])
    for c in range(12):
        for k in range(5):
            col = C_CW + c * 5 + k
            eng = P.dve if (c * 5 + k) % 2 == 0 else P.pool
            eng(lambda e, c=c, k=k, col=col: e.tensor_scalar(out=diagw[:, c, k, :], in0=ident, scalar1=prmA[:, col:col + 1],
                                                             scalar2=None, op0=ALU.mult), r=["prmA"], w=["diagw"])
    for h in range(16):
        P.dve(lambda e, h=h: e.tensor_scalar(out=DI[:, h, :], in0=ident, scalar1=prmA[:, C_DS + h:C_DS + h + 1],
                                             scalar2=None, op0=ALU.mult), r=["prmA"], w=["DI"])
    P.act(lambda e: e.activation(out=aneg[:], in_=prmA[:, C_AL:C_AL + 32], func=AF.Exp), r=["prmA"], w=["aneg"])
    P.dve(lambda e: e.tensor_scalar(out=aneg[:], in0=aneg[:], scalar1=-1.0, scalar2=None, op0=ALU.mult), r=["aneg"], w=["aneg"])
    P.pool(lambda e: e.memset(lrT[:], 1.0), w=["lrT"])
    P.barrier()
    P.sbuf_ptr = markA

    xt = [P.alloc(f"xt{i}", [128, 1024], F32) for i in range(2)]
    xh = [P.alloc(f"xh{i}", [4, 1024], F32) for i in range(2)]
    xT = P.alloc("xT", [128, 8, 132], BF16)
    uT = P.alloc("uT", [128, 12, 132], BF16)
    pack2 = [P.alloc("pack", [128, 4352], BF16) for _ in range(2)]
    xs_tok2 = [pk_[:, 0:1024] for pk_ in pack2]
    B_tok2 = [pk_[:, 1024:1280] for pk_ in pack2]
    BCT2 = [pk_[:, 1280:1792].rearrange("p (c l) -> p c l", c=4) for pk_ in pack2]
    v_bf2 = [pk_[:, 1792:2816] for pk_ in pack2]
    qraw2 = [pk_[:, 2816:3328] for pk_ in pack2]
    kTraw2 = [pk_[:, 3328:3840] for pk_ in pack2]
    ktok2 = [pk_[:, 3840:4352] for pk_ in pack2]
    dtraw2 = [P.alloc("dtraw", [128, 16], F32) for _ in range(2)]
    sm2 = [P.alloc("sm", [128, 128], F32) for _ in range(2)]
    Rt = P.alloc("Rt", [128, 16, 128], F32)
    Et = P.alloc("Et", [128, 16, 128], BF16)
    cbm = P.alloc("cbm", [128, 2, 128], BF16)
    Mt = P.alloc("Mt", [128, 16, 128], BF16)
    xdt = P.alloc("xdt", [128, 1024], BF16)
    xdtd = P.alloc("xdtd", [128, 1024], BF16)
    ytmp = P.alloc("ytmp", [128, 1024], F32)
    yout = [P.alloc(f"yout{i}", [128, 1024], F32) for i in range(2)]
    g_e = P.alloc("g_e", [128, 512], F32)
    g_sp = P.alloc("g_sp", [128, 512], F32)
    g_eb2 = [P.alloc("g_eb", [128, 4, 128], F32) for _ in range(2)]
    g_enb = P.alloc("g_enb", [128, 4, 128], F32)
    g_ed = P.alloc("g_ed", [128, 512], F32)
    qtT2 = [P.alloc("qtT", [128, 4, 128], BF16) for _ in range(2)]
    ktT2 = [P.alloc("ktT", [128, 4, 128], BF16) for _ in range(2)]
    kend2 = [P.alloc("kend", [128, 512], BF16) for _ in range(2)]
    attm = P.alloc("attm", [128, 4, 128], BF16)
    oout = [P.alloc(f"oout{i}", [128, 1024], F32) for i in range(2)]

    def issue_x_load(tok0, seq_lo, seq_hi, slot, halo=True):
        P.dma(xt[slot][:], x_d[tok0:tok0 + 128, :], f"xt{slot}", w=[f"xt{slot}"])
        if not halo:
            return
        hk = f"xh{slot}"
        lo_ok = tok0 - 2 >= seq_lo
        hi_ok = tok0 + 130 <= seq_hi
        if not (lo_ok and hi_ok):
            P.pool(lambda e: e.memset(xh[slot][:], 0.0), w=[hk])
        if lo_ok:
            P.dma(xh[slot][0:2, :], x_d[tok0 - 2:tok0, :], hk, w=[hk])
        if hi_ok:
            P.dma(xh[slot][2:4, :], x_d[tok0 + 128:tok0 + 130, :], hk, w=[hk])

    def transpose_x(slot, dst, c0, identity=ident, idk="prmA"):
        for half in range(2):
            pt, pk = pb()
            for j in range(4):
                kc = half * 4 + j
                P.pe(lambda e, kc=kc, j=j, pt=pt: e.transpose(pt[:, j * 128:(j + 1) * 128], xt[slot][:, kc * 128:(kc + 1) * 128], identity),
                     r=[f"xt{slot}", idk], w=pk)
            src = pt.rearrange("p (c n) -> p c n", n=128)
            if half == 0:
                P.dve(lambda e, src=src: e.tensor_copy(dst[:, 0:4, c0:c0 + 128], src), r=pk, w=["xT"])
            else:
                P.act(lambda e, src=src: e.copy(dst[:, 4:8, c0:c0 + 128], src), r=pk, w=["xT"])

    tile_ctr = {"i": 0}
    Mt2 = [Mt, xt[0][:].bitcast(BF16).rearrange("p (h l) -> p h l", h=16)]
    xdt2 = [xdt, xt[1][:].bitcast(BF16)[:, 0:1024]]
    xdtd2 = [xdtd, xt[1][:].bitcast(BF16)[:, 1024:2048]]
    attm2 = [attm, xT[:].rearrange("p c n -> p (c n)")[:, 0:512].rearrange("p (h l) -> p h l", h=4)]
    wSf = wS[:].rearrange("p a b -> p (a b)")
    def carve(off, n):
        return wSf[:, off:off + n]
    pack3 = carve(0, 4352)
    third = dict(
        sm=carve(4352, 256).bitcast(F32),
        g_eb=carve(4608, 1024).bitcast(F32).rearrange("p (h l) -> p h l", h=4),
        qtT=carve(5632, 512).rearrange("p (h l) -> p h l", h=4),
        ktT=carve(6144, 512).rearrange("p (h l) -> p h l", h=4),
        kend=carve(6656, 512),
        dtraw=carve(7168, 32).bitcast(F32),
        Mt=carve(7232, 2048).rearrange("p (h l) -> p h l", h=16),
        xdt=carve(9280, 1024), xdtd=carve(10304, 1024),
        attm=carve(11328, 512).rearrange("p (h l) -> p h l", h=4))
    g_eF = [g_e, carve(11840, 1024).bitcast(F32)]
    g_spF = [g_sp, carve(12864, 1024).bitcast(F32)]
    g_enbF = [g_enb, carve(13888, 1024).bitcast(F32).rearrange("p (h l) -> p h l", h=4)]
    g_edF = [g_ed, carve(14912, 1024).bitcast(F32)]
    RtF = [Rt, carve(15936, 4096).bitcast(F32).rearrange("p (h l) -> p h l", h=16)]
    EtF = [Et, carve(20032, 2048).rearrange("p (h l) -> p h l", h=16)]
    cbmF = [cbm, carve(22080, 256).rearrange("p (g l) -> p g l", g=2)]
    lrT_b = carve(22336, 256).bitcast(F32)
    lrTF = [lrT, lrT_b]
    pack2.append(pack3)
    xs_tok2.append(pack3[:, 0:1024])
    B_tok2.append(pack3[:, 1024:1280])
    BCT2.append(pack3[:, 1280:1792].rearrange("p (c l) -> p c l", c=4))
    v_bf2.append(pack3[:, 1792:2816])
    qraw2.append(pack3[:, 2816:3328])
    kTraw2.append(pack3[:, 3328:3840])
    ktok2.append(pack3[:, 3840:4352])
    sm2.append(third["sm"]); g_eb2.append(third["g_eb"]); qtT2.append(third["qtT"]); ktT2.append(third["ktT"])
    kend2.append(third["kend"]); dtraw2.append(third["dtraw"])
    Mt2.append(third["Mt"]); xdt2.append(third["xdt"]); xdtd2.append(third["xdtd"]); attm2.append(third["attm"])
    P.parity_keys = {"qraw", "kTraw", "ktokraw", "dtraw", "xs_tok", "B_tok", "BCT", "v_bf", "qtT", "ktT", "kend", "g_eb", "sm_dtv", "sm_dte", "sm_dt", "sm_dta"}

    def scan_front(tok0, slot, d, par, load, fs=0):
        MLE = prmA[:, (C_MLF if d == 0 else C_MLB):(C_MLF if d == 0 else C_MLB) + 128]
        SU = prmA[:, (C_SUF if d == 0 else C_SUB):(C_SUF if d == 0 else C_SUB) + 128]
        TR16 = prmA[:, (C_TRF if d == 0 else C_TRB):(C_TRF if d == 0 else C_TRB) + 128]
        SU16 = prmA[:, (C_S16F if d == 0 else C_S16B):(C_S16F if d == 0 else C_S16B) + 128]
        xs_tok = xs_tok2[par]
        B_tok = B_tok2[par]
        BCT = BCT2[par]
        sm = sm2[par]
        g_eb = g_eb2[par]
        qtT = qtT2[par]
        ktT = ktT2[par]
        kend = kend2[par]
        v_bf = v_bf2[par]
        qraw, kTraw, ktokraw, dtraw = qraw2[par], kTraw2[par], ktok2[par], dtraw2[par]
        g_e, g_sp, g_enb, g_ed, lrT = g_eF[fs], g_spF[fs], g_enbF[fs], g_edF[fs], lrTF[fs]
        tix = tok0 // 128
        PK = ["xs_tok", "B_tok", "BCT", "v_bf", "qraw", "kTraw", "ktokraw"]
        xk, hk = f"xt{slot}", f"xh{slot}"
        if load:
            P.dma(pack2[par][:], packsc[tix], f"pack{par}", w=PK)
            P.dma(lrT[0:16, :], lrsc[tix], f"lrTl{fs}", w=["lrT"])
            P.dma(dtraw[:], dtsc[tix], f"dtraw{par}", w=["dtraw"])
            yield
        else:
            transpose_x(slot, xT, 2)
            pt, pk = pb()
            for kc in range(8):
                P.pe(lambda e, kc=kc, pt=pt: e.transpose(pt[:, kc * 4:(kc + 1) * 4], xh[slot][0:4, kc * 128:(kc + 1) * 128], ident[0:4, 0:4]),
                     r=[hk, "prmA"], w=pk)
            hv = pt[:, 0:32].rearrange("p (c n) -> p c n", n=4)
            P.dve(lambda e, hv=hv: e.tensor_copy(xT[:, :, 0:2], hv[:, :, 0:2]), r=pk, w=["xT"])
            P.dve(lambda e, hv=hv: e.tensor_copy(xT[:, :, 130:132], hv[:, :, 2:4]), r=pk, w=["xT"])
            yield
            for grp in range(4):
                pt, pk = pb()
                for j in range(3):
                    c = grp * 3 + j
                    for kc in range(8):
                        P.pe(lambda e, c=c, kc=kc, j=j, pt=pt: e.matmul(pt[:, j * 132:(j + 1) * 132], wS[:, kc, c * 128:(c + 1) * 128],
                                                                       xT[:, kc, :], start=(kc == 0), stop=(kc == 7)),
                             r=["wS", "xT"], w=pk)
                src = pt[:, 0:396].rearrange("p (c n) -> p c n", n=132)
                if grp % 2 == 0:
                    P.act(lambda e, src=src, grp=grp: e.copy(uT[:, grp * 3:grp * 3 + 3, :], src), r=pk, w=["uT"])
                else:
                    P.dve(lambda e, src=src, grp=grp: e.tensor_copy(uT[:, grp * 3:grp * 3 + 3, :], src), r=pk, w=["uT"])
                yield
            psml, psmlk = pb()
            for kc in range(8):
                P.pe(lambda e, kc=kc: e.matmul(psml[0:16, 0:128], wS[:, kc, 3616:3632], xT[:, kc, 2:130], start=(kc == 0), stop=(kc == 7)),
                     r=["wS", "xT"], w=psmlk)
            P.act(lambda e: e.copy(lrT[0:16, :], psml[0:16, 0:128]), r=psmlk, w=["lrT"])
            yield
            P.dma(lrsc[tix], lrT[0:16, :], "lrTs", r=["lrT"])
        pg, pgk = pb()
        P.pe(lambda e: e.matmul(pg[:, 0:512], lrT[0:17, :], wgk[0:17, 512 * d:512 * d + 512], start=True, stop=True),
             r=["lrT", "wgk"], w=pgk)
        P.act(lambda e: e.activation(out=g_e[:], in_=pg[:, 0:512], func=AF.Exp, scale=-1.0), r=pgk, w=["g_e"])
        P.act(lambda e: e.activation(out=g_sp[:], in_=g_e[:], func=AF.Ln, bias=1.0), r=["g_e"], w=["g_sp"])
        yield
        pbT, pbTk = pb()
        for h in range(4):
            P.pe(lambda e, h=h: e.matmul(pbT[:, h * 128:(h + 1) * 128], g_sp[:, h * 128:(h + 1) * 128], TR16, start=True, stop=True),
                 r=["g_sp", "prmA"], w=pbTk)
        pde, pdek = pb()
        P.pe(lambda e: e.matmul(pde[:, 0:512], SU16, g_sp[:], start=True, stop=True), r=["g_sp", "prmA"], w=pdek)
        P.act(lambda e: e.activation(out=g_eb[:].rearrange("p h l -> p (h l)"), in_=pbT[:, 0:512], func=AF.Exp), r=pbTk, w=["g_eb"])
        P.act(lambda e: e.activation(out=g_enb[:].rearrange("p h l -> p (h l)"), in_=pbT[:, 0:512], func=AF.Exp, scale=-1.0), r=pbTk, w=["g_enb"])
        P.act(lambda e: e.activation(out=g_ed[:], in_=pde[:, 0:512], func=AF.Exp), r=pdek, w=["g_ed"])
        yield
        dtv, dte, dtt, dta = sm[:, 0:16], sm[:, 16:32], sm[:, 32:48], sm[:, 48:64]
        if load:
            P.dve(lambda e: e.scalar_tensor_tensor(out=qtT[:].rearrange("p h l -> p (h l)"), in0=qraw, scalar=128.0 ** -0.5,
                                                   in1=g_eb[:].rearrange("p h l -> p (h l)"), op0=ALU.mult, op1=ALU.mult),
                  r=["qraw", "g_eb"], w=["qtT"])
            P.pool(lambda e: e.tensor_tensor(out=ktT[:].rearrange("p h l -> p (h l)"), in0=kTraw,
                                             in1=g_enb[:].rearrange("p h l -> p (h l)"), op=ALU.mult), r=["kTraw", "g_enb"], w=["ktT"])
            P.dve(lambda e: e.tensor_tensor(out=kend[:], in0=ktokraw, in1=g_ed[:], op=ALU.mult), r=["ktokraw", "g_ed"], w=["kend"])
            yield
            P.dve(lambda e: e.tensor_tensor(out=dtv, in0=dtraw[:], in1=prmA[:, C_DTB + 16 * d:C_DTB + 16 * d + 16], op=ALU.add),
                  r=["dtraw", "prmA"], w=["sm_dtv"])
        else:
            pq, pqk = pb()
            for h in range(4):
                for kc in range(8):
                    P.pe(lambda e, h=h, kc=kc: e.matmul(pq[:, h * 128:(h + 1) * 128], wS[:, kc, 1568 + h * 128:1568 + (h + 1) * 128],
                                                       xT[:, kc, 2:130], start=(kc == 0), stop=(kc == 7)), r=["wS", "xT"], w=pqk)
            P.dve(lambda e: e.scalar_tensor_tensor(out=qtT[:].rearrange("p h l -> p (h l)"), in0=pq[:, 0:512], scalar=128.0 ** -0.5,
                                                   in1=g_eb[:].rearrange("p h l -> p (h l)"), op0=ALU.mult, op1=ALU.mult),
                  r=pqk + ["g_eb"], w=["qtT"])
            P.act(lambda e: e.copy(qraw, pq[:, 0:512]), r=pqk, w=["qraw"])
            yield
            pkT, pkTk = pb()
            for h in range(4):
                for kc in range(8):
                    P.pe(lambda e, h=h, kc=kc: e.matmul(pkT[:, h * 128:(h + 1) * 128], wS[:, kc, 2080 + h * 128:2080 + (h + 1) * 128],
                                                       xT[:, kc, 2:130], start=(kc == 0), stop=(kc == 7)), r=["wS", "xT"], w=pkTk)
            P.dve(lambda e: e.tensor_tensor(out=ktT[:].rearrange("p h l -> p (h l)"), in0=pkT[:, 0:512],
                                            in1=g_enb[:].rearrange("p h l -> p (h l)"), op=ALU.mult), r=pkTk + ["g_enb"], w=["ktT"])
            P.act(lambda e: e.copy(kTraw, pkT[:, 0:512]), r=pkTk, w=["kTraw"])
            yield
            pkt, pktk = pb()
            for kc in range(8):
                P.pe(lambda e, kc=kc: e.matmul(pkt[:, 0:512], xT[:, kc, 2:130], wS[:, kc, 2080:2592], start=(kc == 0), stop=(kc == 7)),
                     r=["wS", "xT"], w=pktk)
            P.dve(lambda e: e.tensor_tensor(out=kend[:], in0=pkt[:, 0:512], in1=g_ed[:], op=ALU.mult), r=pktk + ["g_ed"], w=["kend"])
            P.act(lambda e: e.copy(ktokraw, pkt[:, 0:512]), r=pktk, w=["ktokraw"])
            yield
            pdt, pdtk = pb()
            for kc in range(8):
                P.pe(lambda e, kc=kc: e.matmul(pdt[:, 0:32], xT[:, kc, 2:130], wS[:, kc, 1536:1568], start=(kc == 0), stop=(kc == 7)),
                     r=["wS", "xT"], w=pdtk)
            P.dve(lambda e: e.tensor_tensor(out=dtv, in0=pdt[:, 16 * d:16 * d + 16], in1=prmA[:, C_DTB + 16 * d:C_DTB + 16 * d + 16], op=ALU.add),
                  r=pdtk + ["prmA"], w=["sm_dtv"])
            P.act(lambda e: e.copy(dtraw[:], pdt[:, 16 * (1 - d):16 * (1 - d) + 16]), r=pdtk, w=["dtraw"])
            P.dma(dtsc[tix], dtraw[:], f"dtraws{par}", r=["dtraw"])
        P.act(lambda e: e.activation(out=dte, in_=dtv, func=AF.Exp), r=["sm_dtv"], w=["sm_dte"])
        P.act(lambda e: e.activation(out=dtt, in_=dte, func=AF.Ln, bias=1.0), r=["sm_dte"], w=["sm_dt"])
        P.dve(lambda e: e.tensor_tensor(out=dta, in0=dtt, in1=aneg[:, 16 * d:16 * d + 16], op=ALU.mult), r=["sm_dt", "aneg"], w=["sm_dta"])
        if not load:
            yield
            pv, pvk = pb(2)
            for half in range(2):
                for kc in range(8):
                    P.pe(lambda e, kc=kc, half=half: e.matmul(pv[:, half * 512:(half + 1) * 512], xT[:, kc, 2:130],
                                                             wS[:, kc, 2592 + half * 512:2592 + (half + 1) * 512], start=(kc == 0), stop=(kc == 7)),
                         r=["wS", "xT"], w=[pvk[half]])
            P.act(lambda e: e.copy(v_bf[:], pv), r=pvk, w=["v_bf"])
            yield
            pcx, pcxk = pb(2)
            for c in range(8):
                o = pcx[:, c * 128:(c + 1) * 128]
                for k in range(5):
                    P.pe(lambda e, o=o, c=c, k=k: e.matmul(o, uT[:, c, k:k + 128], diagw[:, c, k, :], start=(k == 0), stop=False),
                         r=["uT", "diagw"], w=[pcxk[c // 4]])
                P.pe(lambda e, o=o, c=c: e.matmul(o, onesb[0:1, :], cbrow[0:1, c * 128:(c + 1) * 128], start=False, stop=True),
                     r=["onesb", "cbrow"], w=[pcxk[c // 4]])
            yield
            pcb, pcbk = pb()
            for c in range(8, 10):
                o = pcb[:, (c - 8) * 128:(c - 7) * 128]
                for k in range(5):
                    P.pe(lambda e, o=o, c=c, k=k: e.matmul(o, uT[:, c, k:k + 128], diagw[:, c, k, :], start=(k == 0), stop=False),
                         r=["uT", "diagw"], w=pcbk)
                P.pe(lambda e, o=o, c=c: e.matmul(o, onesb[0:1, :], cbrow[0:1, c * 128:(c + 1) * 128], start=False, stop=True),
                     r=["onesb", "cbrow"], w=pcbk)
            pct, pctk = pb()
            for c in range(8, 12):
                o = pct[:, (c - 8) * 128:(c - 7) * 128]
                for k in range(5):
                    P.pe(lambda e, o=o, c=c, k=k: e.matmul(o, diagw[:, c, k, :], uT[:, c, k:k + 128], start=(k == 0), stop=(k == 4)),
                         r=["uT", "diagw"], w=pctk)
            yield
            P.act(lambda e: e.activation(out=xs_tok[:], in_=pcx, func=AF.Silu), r=pcxk, w=["xs_tok"])
            P.act(lambda e: e.activation(out=B_tok[:], in_=pcb[:, 0:256], func=AF.Silu), r=pcbk, w=["B_tok"])
            for c in range(8, 12):
                P.act(lambda e, c=c: e.activation(out=BCT[:, c - 8, :], in_=pct[:, (c - 8) * 128:(c - 7) * 128], func=AF.Silu,
                                                  bias=prmA[:, C_CB + c:C_CB + c + 1]), r=pctk + ["prmA"], w=["BCT"])
            P.dma(packsc[tix], pack2[par][:], f"packs{par}", r=PK)

    def scan_back(tok0, slot, d, par, add_skip, part=None, bpar=0, fs=0):
        MLE = prmA[:, (C_MLF if d == 0 else C_MLB):(C_MLF if d == 0 else C_MLB) + 128]
        SU = prmA[:, (C_SUF if d == 0 else C_SUB):(C_SUF if d == 0 else C_SUB) + 128]
        TR16 = prmA[:, (C_TRF if d == 0 else C_TRB):(C_TRF if d == 0 else C_TRB) + 128]
        SU16 = prmA[:, (C_S16F if d == 0 else C_S16B):(C_S16F if d == 0 else C_S16B) + 128]
        xs_tok = xs_tok2[par]
        B_tok = B_tok2[par]
        BCT = BCT2[par]
        sm = sm2[par]
        g_eb = g_eb2[par]
        qtT = qtT2[par]
        ktT = ktT2[par]
        kend = kend2[par]
        v_bf = v_bf2[par]
        dtv, dte, dtt, dta = sm[:, 0:16], sm[:, 16:32], sm[:, 32:48], sm[:, 48:64]
        Mt, xdt, xdtd, attm = Mt2[bpar], xdt2[bpar], xdtd2[bpar], attm2[bpar]
        Rt, Et, cbm = RtF[fs], EtF[fs], cbmF[fs]

        def part_a():
            pss, pssk = pb()
            P.pe(lambda e: e.matmul(pss[:, 0:16], SU, dta, start=True, stop=True), r=["prmA", "sm_dta"], w=pssk)
            P.pe(lambda e: e.matmul(pss[:, 16:32], MLE, dta, start=True, stop=True), r=["prmA", "sm_dta"], w=pssk)
            P.pe(lambda e: e.matmul(pss[:, 32:48], ones, dta, start=True, stop=True), r=["prmA", "sm_dta"], w=pssk)
            ex = sm[:, 64:112]
            P.act(lambda e: e.activation(out=ex, in_=pss[:, 0:48], func=AF.Exp), r=pssk, w=["sm_ex"])
            wdec = sm[:, 112:128]
            P.dve(lambda e: e.tensor_tensor(out=wdec, in0=dtt, in1=sm[:, 64:80], op=ALU.mult), r=["sm_dt", "sm_ex"], w=["sm_w"])
            yield
            P.dve(lambda e: e.tensor_tensor(out=Rt[:], in0=dta.unsqueeze(2).to_broadcast([128, 16, 128]),
                                            in1=MLE.unsqueeze(1).to_broadcast([128, 16, 128]), op=ALU.mult),
                  r=["sm_dta", "prmA"], w=["Rt"])
            yield
            pcbT, pcbTk = pb()
            for g in range(2):
                P.pe(lambda e, g=g: e.matmul(pcbT[:, g * 128:(g + 1) * 128], BCT[:, g, :], BCT[:, 2 + g, :], start=True, stop=True),
                     r=["BCT"], w=pcbTk)
            P.dve(lambda e: e.tensor_tensor(out=cbm[:], in0=pcbT[:, 0:256].rearrange("p (g l) -> p g l", g=2),
                                            in1=MLE.unsqueeze(1).to_broadcast([128, 2, 128]), op=ALU.mult),
                  r=pcbTk + ["prmA"], w=["cbm"])
            yield
            P.dve(lambda e: e.tensor_tensor(out=xdt[:].rearrange("p (h c) -> p h c", c=64), in0=xs_tok[:].rearrange("p (h c) -> p h c", c=64),
                                            in1=dtt.unsqueeze(2).to_broadcast([128, 16, 64]), op=ALU.mult),
                  r=["xs_tok", "sm_dt"], w=["xdt"])
            P.pool(lambda e: e.tensor_tensor(out=xdtd[:].rearrange("p (h c) -> p h c", c=64), in0=xs_tok[:].rearrange("p (h c) -> p h c", c=64),
                                             in1=wdec.unsqueeze(2).to_broadcast([128, 16, 64]), op=ALU.mult),
                   r=["xs_tok", "sm_w"], w=["xdtd"])
            yield
            for half in range(2):
                psg, psgk = pb(2)
                for q in range(2):
                    hh = half * 8 + q * 4
                    P.pe(lambda e, q=q, hh=hh, psg=psg: e.matmul(psg[:, q * 512:(q + 1) * 512], SU,
                                                                Rt[:, hh:hh + 4, :].rearrange("p h l -> p (h l)"), start=True, stop=True),
                         r=["prmA", "Rt"], w=[psgk[q]])
                P.act(lambda e, half=half, psg=psg: e.activation(out=Et[:, half * 8:(half + 1) * 8, :].rearrange("p h l -> p (h l)"), in_=psg,
                                                                func=AF.Exp), r=psgk, w=[f"Et{half}"])
                (P.pool if (d == 0 and half == 1) else P.dve)(lambda e, half=half: e.tensor_tensor(out=Mt[:, half * 8:(half + 1) * 8, :], in0=Et[:, half * 8:(half + 1) * 8, :],
                                                           in1=cbm[:, half, :].unsqueeze(1).to_broadcast([128, 8, 128]), op=ALU.mult),
                      r=[f"Et{half}", "cbm"], w=[f"Mt{half}"])
                yield
            yield
            pat, patk = pb()
            for h in range(4):
                P.pe(lambda e, h=h: e.matmul(pat[:, h * 128:(h + 1) * 128], ktT[:, h, :], qtT[:, h, :], start=True, stop=True),
                     r=["ktT", "qtT"], w=patk)
            P.dve(lambda e: e.tensor_tensor(out=attm[:], in0=pat[:, 0:512].rearrange("p (h l) -> p h l", h=4),
                                            in1=MLE.unsqueeze(1).to_broadcast([128, 4, 128]), op=ALU.mult), r=patk + ["prmA"], w=["attm"])

        def part_b1():
            yield
            pyo, pyok = pb(2)
            for g in range(2):
                P.pe(lambda e, g=g: e.matmul(pyo[:, g * 512:(g + 1) * 512], BCT[:, 2 + g, :], hTb[:, g * 512:(g + 1) * 512], start=True, stop=True),
                     r=["BCT", "hTb"], w=[pyok[g]])
            eacs = sm[:, 80:96]
            P.dve(lambda e: e.tensor_tensor(out=ytmp[:].rearrange("p (h c) -> p h c", c=64), in0=pyo.rearrange("p (h c) -> p h c", c=64),
                                            in1=eacs.unsqueeze(2).to_broadcast([128, 16, 64]), op=ALU.mult),
                  r=pyok + ["sm_ex"], w=["ytmp"])
            yield
            pyd, pydk = pb(2)
            for h in range(16):
                o = pyd[:, h * 64:(h + 1) * 64]
                P.pe(lambda e, o=o, h=h: e.matmul(o, Mt[:, h, :], xdt[:, h * 64:(h + 1) * 64], start=True, stop=not add_skip),
                     r=[f"Mt{h // 8}", "xdt"], w=[pydk[h // 8]])
                if add_skip:
                    P.pe(lambda e, o=o, h=h: e.matmul(o, DI[:, h, :], xs_tok[:, h * 64:(h + 1) * 64], start=False, stop=True),
                         r=["DI", "xs_tok"], w=[pydk[h // 8]])
            yield
            yo = yout[slot]
            yk = f"yout{slot}"
            P.dve(lambda e: e.tensor_tensor(out=yo[:], in0=ytmp[:], in1=pyd, op=ALU.add), r=["ytmp"] + pydk, w=[yk])
            P.dma(ysc[d, tok0:tok0 + 128, :], yo[:], yk + "s", r=[yk])
            yield
            pcs, pcsk = pb(2)
            for g in range(2):
                P.pe(lambda e, g=g: e.matmul(pcs[:, g * 512:(g + 1) * 512], B_tok[:, g * 128:(g + 1) * 128], xdtd[:, g * 512:(g + 1) * 512],
                                             start=True, stop=True), r=["B_tok", "xdtd"], w=[pcsk[g]])
            dec = sm[:, 96:112]
            (P.pool if d == 0 else P.dve)(lambda e: e.tensor_tensor(out=hT[:].rearrange("p (h c) -> p h c", c=64), in0=hT[:].rearrange("p (h c) -> p h c", c=64),
                                            in1=dec.unsqueeze(2).to_broadcast([128, 16, 64]), op=ALU.mult),
                  r=["sm_ex"], w=["hT"])
            P.dve(lambda e: e.tensor_tensor(out=hT[:], in0=hT[:], in1=pcs, op=ALU.add), r=pcsk, w=["hT"])
            P.act(lambda e: e.copy(hTb[:], hT[:]), r=["hT"], w=["hTb"])

        def part_b2():
            yield
            po, pok = pb(2)
            for h in range(4):
                o = po[:, h * 256:(h + 1) * 256]
                P.pe(lambda e, o=o, h=h: e.matmul(o, attm[:, h, :], v_bf[:, h * 256:(h + 1) * 256], start=True, stop=False),
                     r=["attm", "v_bf"], w=[pok[h // 2]])
                P.pe(lambda e, o=o, h=h: e.matmul(o, qtT[:, h, :], Hsb[:, h, :], start=False, stop=True),
                     r=["qtT", "Hsb"], w=[pok[h // 2]])
            oo = oout[slot]
            ok = f"oout{slot}"
            P.act(lambda e: e.copy(oo[:], po), r=pok, w=[ok])
            P.dma(osc[d, tok0:tok0 + 128, :], oo[:], ok + "s", r=[ok])
            yield
            pgs, pgsk = pb(2)
            for h in range(4):
                P.pe(lambda e, h=h: e.matmul(pgs[:, h * 256:(h + 1) * 256], kend[:, h * 128:(h + 1) * 128], v_bf[:, h * 256:(h + 1) * 256],
                                             start=True, stop=True), r=["kend", "v_bf"], w=[pgsk[h // 2]])
            last = 127 if d == 0 else 0
            for h in range(4):
                P.dve(lambda e, h=h: e.scalar_tensor_tensor(out=Hs[:, h, :], in0=Hs[:, h, :], scalar=g_eb[:, h, last:last + 1],
                                                            in1=pgs[:, h * 256:(h + 1) * 256], op0=ALU.mult, op1=ALU.add),
                      r=["g_eb", pgsk[h // 2]], w=["Hs"])
            P.act(lambda e: e.copy(Hsb[:].rearrange("p h v -> p (h v)"), Hs[:].rearrange("p h v -> p (h v)")), r=["Hs"], w=["Hsb"])


        if part in (None, "a"):
            yield from part_a()
        if part in (None, "b", "b1"):
            yield from part_b1()
        if part in (None, "b", "b2"):
            yield from part_b2()

    seq_offs = [sum(seq_lens[:i]) for i in range(NS)]
    def chain(*gs):
        for g_ in gs:
            yield from g_

    for d in (1, 0):
        if d == 0:
            P.barrier()
            pstate["groups"] = [(0, 2), (2, 2), (4, 2), (6, 2)]
            pstate["gi"] = [0, 0, 0, 0]
            P.fset_keys = {"g_e", "g_sp", "g_enb", "g_ed", "lrT", "Rt", "Et0", "Et1", "cbm"}
            P.pool(lambda e: e.memset(lrT_b[0:32, :], 1.0), w=["lrT$1"])
            P.parity_keys = P.parity_keys | {"Mt0", "Mt1", "xdt", "xdtd", "attm", "sm_ex", "sm_w"}
        for si in range(NS):
            L = seq_lens[si]
            so = seq_offs[si]
            nt = L // 128
            order = list(range(nt)) if d == 0 else list(range(nt - 1, -1, -1))
            P.pool(lambda e: e.memset(hT[:], 0.0), w=["hT"])
            P.pool(lambda e: e.memset(hTb[:], 0.0), w=["hTb"])
            P.pool(lambda e: e.memset(Hs[:].rearrange("p h v -> p (h v)"), 0.0), w=["Hs"])
            P.pool(lambda e: e.memset(Hsb[:].rearrange("p h v -> p (h v)"), 0.0), w=["Hsb"])
            base = tile_ctr["i"]
            if d == 1:
                issue_x_load(so + order[0] * 128, so, so + L, base % 2)
                if nt > 1:
                    issue_x_load(so + order[1] * 128, so, so + L, (base + 1) % 2)
                run_pipe([(scan_front(so + order[0] * 128, base % 2, d, base % 2, False), 0, base % 2)])
                for j, ti in enumerate(order):
                    slot = (base + j) % 2
                    gens = [(scan_back(so + ti * 128, slot, d, slot, False), 1, slot)]
                    if j + 1 < nt:
                        if j + 2 < nt:
                            issue_x_load(so + order[j + 2] * 128, so, so + L, slot)
                        ns = (base + j + 1) % 2
                        gens.append((scan_front(so + order[j + 1] * 128, ns, d, ns, False), 0, ns))
                    run_pipe(gens)
            else:
                doneF, doneB1, doneB2 = {}, {}, {}

                def seqF(hold, q, so=so, nt=nt):
                    for j in range(q, nt, 2):
                        while j >= 3 and not (doneB1.get(j - 3) and doneB2.get(j - 3)):
                            yield
                        hold["par"] = j % 3
                        yield
                        t = so + j * 128
                        yield from scan_front(t, j % 3, 0, j % 3, True, q)
                        yield from scan_back(t, j % 3, 0, j % 3, True, "a", j % 3, q)
                        doneF[j] = True

                def seqB(hold, part, done, so=so, nt=nt):
                    for j in range(nt):
                        while not doneF.get(j):
                            yield
                        hold["par"] = j % 3
                        yield
                        yield from scan_back(so + j * 128, j % 2, 0, j % 3, True, part, j % 3)
                        done[j] = True

                hF0, hF1, h1, h2 = {"par": 0, "fset": 0}, {"par": 0, "fset": 1}, {"par": 0}, {"par": 0}
                run_pipe([(seqB(h1, "b1", doneB1), 2, h1), (seqB(h2, "b2", doneB2), 3, h2),
                          (seqF(hF0, 0), 0, hF0), (seqF(hF1, 1), 1, hF1)])
            tile_ctr["i"] = base + nt

    def rsqrt_small(dst, src, eps, key):
        P.act(lambda e: e.activation(out=dst, in_=src, func=AF.Ln, bias=eps), r=[key], w=[key + "r"])
        P.act(lambda e: e.activation(out=dst, in_=dst, func=AF.Exp, scale=-0.5), r=[key + "r"], w=[key + "r"])

    def layer_norm(src, skey, g_ap, b_ap, gkey, dst, dkey, stt, stkey):
        P.dve(lambda e: e.bn_stats(stt[:, 0:6], src[:, 0:512]), r=[skey], w=[stkey])
        P.dve(lambda e: e.bn_stats(stt[:, 6:12], src[:, 512:1024]), r=[skey], w=[stkey])
        P.dve(lambda e: e.bn_aggr(stt[:, 12:14], stt[:, 0:12].rearrange("p (a b) -> p a b", b=6)), r=[stkey], w=[stkey])
        rsqrt_small(stt[:, 14:15], stt[:, 13:14], LN_EPS, stkey)
        P.dve(lambda e: e.scalar_tensor_tensor(out=src, in0=src, scalar=stt[:, 12:13], in1=g_ap, op0=ALU.subtract, op1=ALU.mult),
              r=[skey, stkey, gkey], w=[skey])
        P.dve(lambda e: e.scalar_tensor_tensor(out=dst, in0=src, scalar=stt[:, 14:15], in1=b_ap, op0=ALU.mult, op1=ALU.add),
              r=[skey, stkey + "r", gkey], w=[dkey])

    def transpose_f32(src, skey, dst, dkey, identity, idk, c0=0):
        for half in range(2):
            pt, pk = pb()
            for j in range(4):
                kc = half * 4 + j
                P.pe(lambda e, kc=kc, j=j, pt=pt: e.transpose(pt[:, j * 128:(j + 1) * 128], src[:, kc * 128:(kc + 1) * 128], identity),
                     r=[skey, idk], w=pk)
            sv = pt.rearrange("p (c n) -> p c n", n=128)
            if half == 0:
                P.dve(lambda e, sv=sv: e.tensor_copy(dst[:, 0:4, c0:c0 + 128], sv), r=pk, w=[dkey + "h0"])
            else:
                P.act(lambda e, sv=sv: e.copy(dst[:, 4:8, c0:c0 + 128], sv), r=pk, w=[dkey + "h1"])

    P.parity_keys = set()
    P.fset_keys = set()
    x1sc = x2sc
    x2sc2 = nc.dram_tensor("x2sc2", [LT, D], F32).ap()
    P.barrier()
    P.sbuf_ptr = base_ptr
    NM = MT // 128
    KA = 4
    wZG = P.alloc("wZG", [128, 8, 2048], BF16)
    wO = P.alloc("wO", [128, 16, 1024], BF16)
    prmB = P.alloc("prmB", [128, 4 * 1024], F32)
    identB = P.alloc("identB", [128, 128], F32)
    identBb = P.alloc("identBb", [128, 128], BF16)
    markB = P.sbuf_ptr
    stg = {"i": 0, "t": [P.alloc(f"stgB{i}", [128, 2048], F32) for i in range(6)]}
    P.dma(prmB[:], prmB_d[:, 0:4 * 1024], "G:B", w=["prmB"])
    P.dma(identB[:], prmA_d[:, C_ID:C_ID + 128], "G:B", w=["identB"])
    P.dve(lambda e: e.tensor_copy(identBb[:], identB[:]), r=["identB"], w=["identBb"])
    load_weight(wZG, w_in_d, 8, 0, 1024, 0, stg, "wZG")
    load_weight(wZG, w_in_d, 8, 4640, 1024, 1024, stg, "wZG")
    load_weight(wO, w_out_d, 16, 0, 1024, 0, stg, "wO")
    P.barrier()
    P.sbuf_ptr = markB
    BA = []
    for k in range(KA):
        BA.append(dict(
            xtB=P.alloc("xtB", [128, 1024], F32), xTb=P.alloc("xTb", [128, 8, 128], BF16), szg=P.alloc("szg", [128, 2048], BF16),
            yfb=P.alloc("yfb", [128, 1024], F32), ybb=P.alloc("ybb", [128, 1024], F32), ofb=P.alloc("ofb", [128, 1024], F32),
            obb=P.alloc("obb", [128, 1024], F32), mix=P.alloc("mix", [128, 2048], BF16), st2=P.alloc("st2", [128, 32], F32)))
        BA[-1]["mixT"] = BA[-1]["obb"][:].bitcast(BF16).rearrange("p (c n) -> p c n", n=128)
        BA[-1]["junk"] = BA[-1]["xTb"][:].rearrange("p c n -> p (c n)")
    G_SSD, G_GLA, G_L1G, G_L1B = [prmB[:, i * 1024:(i + 1) * 1024] for i in range(4)]
    P.shared_keys = {"wZG", "wO", "prmB", "identB", "identBb"}
    pstate["groups"] = [(0, 2), (2, 2), (4, 2), (6, 2)]
    pstate["gi"] = [0, 0, 0, 0]

    def b2a_tile(tok0, k):
        B = BA[k]
        xtt, xTb, szg, yfb, ybb, ofb, obb, mix, mixT, junk, st2 = (B[n] for n in
            ("xtB", "xTb", "szg", "yfb", "ybb", "ofb", "obb", "mix", "mixT", "junk", "st2"))
        P.dma(xtt[:], x_d[tok0:tok0 + 128, :], "xtB", w=["xtB"])
        P.dma(yfb[:], ysc[0, tok0:tok0 + 128, :], "yfb", w=["yfb"])
        P.dma(ybb[:], ysc[1, tok0:tok0 + 128, :], "ybb", w=["ybb"])
        P.dma(ofb[:], osc[0, tok0:tok0 + 128, :], "ofb", w=["ofb"])
        P.dma(obb[:], osc[1, tok0:tok0 + 128, :], "obb", w=["obb", "mixT0", "mixT1"])
        yield
        transpose_f32(xtt, "xtB", xTb, "xTb", identB[:], "identB")
        yield
        for blk in range(4):
            pt, pk = pb()
            for kc in range(8):
                P.pe(lambda e, kc=kc, blk=blk, pt=pt: e.matmul(pt, xTb[:, kc, :], wZG[:, kc, blk * 512:(blk + 1) * 512],
                                                              start=(kc == 0), stop=(kc == 7)), r=["xTbh0", "xTbh1", "wZG"], w=pk)
            P.act(lambda e, blk=blk, pt=pt: e.activation(out=szg[:, blk * 512:(blk + 1) * 512], in_=pt, func=AF.Silu), r=pk, w=[f"szg{blk}"])
            yield
        P.pool(lambda e: e.memset(st2[:, 0:8], 0.0), w=["st2a", "st2c"])
        P.dve(lambda e: e.tensor_tensor(out=yfb[:], in0=yfb[:], in1=ybb[:], op=ALU.add), r=["yfb", "ybb"], w=["yfb"])
        P.dve(lambda e: e.tensor_tensor(out=ybb[:], in0=yfb[:], in1=szg[:, 0:1024], op=ALU.mult), r=["yfb", "szg0", "szg1"], w=["ybb"])
        for g in range(2):
            P.act(lambda e, g=g: e.activation(out=junk[:, 0:512], in_=ybb[:, g * 512:(g + 1) * 512], func=AF.Square, accum_out=st2[:, g:g + 1]),
                  r=["ybb"], w=["xTbh0", "xTbh1", "st2a"])
        yield
        P.dve(lambda e: e.tensor_scalar(out=st2[:, 8:10], in0=st2[:, 0:2], scalar1=1.0 / 512, scalar2=None, op0=ALU.mult),
              r=["st2a"], w=["st2b"])
        rsqrt_small(st2[:, 8:10], st2[:, 8:10], RMS_EPS, "st2b")
        for g in range(2):
            P.dve(lambda e, g=g: e.scalar_tensor_tensor(out=mix[:, g * 512:(g + 1) * 512], in0=ybb[:, g * 512:(g + 1) * 512],
                                                        scalar=st2[:, 8 + g:9 + g], in1=G_SSD[:, g * 512:(g + 1) * 512],
                                                        op0=ALU.mult, op1=ALU.mult), r=["ybb", "st2br", "prmB"], w=["mixa"])
        yield
        P.pool(lambda e: e.tensor_tensor(out=ofb[:], in0=ofb[:], in1=obb[:], op=ALU.add), r=["ofb", "obb"], w=["ofb"])
        for h in range(4):
            P.act(lambda e, h=h: e.activation(out=junk[:, 0:256], in_=ofb[:, h * 256:(h + 1) * 256], func=AF.Square, accum_out=st2[:, 2 + h:3 + h]),
                  r=["ofb"], w=["xTbh0", "xTbh1", "st2c"])
        yield
        P.dve(lambda e: e.tensor_scalar(out=st2[:, 12:16], in0=st2[:, 2:6], scalar1=1.0 / 256, scalar2=None, op0=ALU.mult),
              r=["st2c"], w=["st2d"])
        rsqrt_small(st2[:, 12:16], st2[:, 12:16], RMS_EPS, "st2d")
        for h in range(4):
            P.dve(lambda e, h=h: e.scalar_tensor_tensor(out=obb[:, h * 256:(h + 1) * 256], in0=ofb[:, h * 256:(h + 1) * 256],
                                                        scalar=st2[:, 12 + h:13 + h], in1=G_GLA[:, h * 256:(h + 1) * 256],
                                                        op0=ALU.mult, op1=ALU.mult), r=["ofb", "st2dr", "prmB"], w=["obb"])
        P.pool(lambda e: e.tensor_tensor(out=mix[:, 1024:2048], in0=obb[:], in1=szg[:, 1024:2048], op=ALU.mult),
               r=["obb", "szg2", "szg3"], w=["mixb"])
        yield
        for half in range(2):
            pt, pk = pb()
            ptb = pt.bitcast(BF16)
            for j in range(8):
                c = half * 8 + j
                P.pe(lambda e, c=c, j=j, ptb=ptb: e.transpose(ptb[:, j * 128:(j + 1) * 128], mix[:, c * 128:(c + 1) * 128], identBb[:]),
                     r=["mixa" if c < 8 else "mixb", "identBb"], w=pk)
            sv = ptb.rearrange("p (c n) -> p c n", n=128)
            if half == 0:
                P.act(lambda e, sv=sv: e.copy(mixT[:, 0:8, :], sv), r=pk + ["mixb"], w=["mixT0", "obb"])
            else:
                P.dve(lambda e, sv=sv: e.tensor_copy(mixT[:, 8:16, :], sv), r=pk + ["mixb"], w=["mixT1", "obb"])
            yield
        po1, po1k = pb(2)
        for half in range(2):
            for c in range(16):
                P.pe(lambda e, c=c, half=half: e.matmul(po1[:, half * 512:(half + 1) * 512], mixT[:, c, :], wO[:, c, half * 512:(half + 1) * 512],
                                                       start=(c == 0), stop=(c == 15)), r=[f"mixT{c // 8}", "wO"], w=[po1k[half]])
            yield
        P.dve(lambda e: e.scalar_tensor_tensor(out=yfb[:], in0=xtt[:], scalar=ALPHA, in1=po1, op0=ALU.mult, op1=ALU.add),
              r=["xtB"] + po1k, w=["yfb"])
        yield
        layer_norm(yfb[:], "yfb", G_L1G, G_L1B, "prmB", ybb[:], "ybb", st2[:, 16:32], "st2e")
        P.dma(x1sc[tok0:tok0 + 128, :], ybb[:], "ybbs", r=["ybb"])

    tiles = []
    for si in range(NS):
        for ti in range(seq_lens[si] // 128):
            tiles.append((seq_offs[si] + ti * 128, si))
    run_streams(tiles, lambda it, k: b2a_tile(it[0], k), KA, 4)

    P.shared_keys = set()
    P.barrier()
    P.sbuf_ptr = base_ptr
    KB = 8
    wQ = P.alloc("wQ", [128, 8, 1024], BF16)
    wMO = P.alloc("wMO", [128, 8, 1024], BF16)
    prmB2 = P.alloc("prmB2", [128, 2 * 1024], F32)
    identQ = P.alloc("identB2", [128, 128], F32)
    identQb = P.alloc("identBb2", [128, 128], BF16)
    KT_all = [P.alloc(f"KT{i}", [128, 8, MT], BF16) for i in range(NS)]
    Vm_all = [P.alloc(f"Vm{i}", [128, NM, 1024], BF16) for i in range(NS)]
    markB2 = P.sbuf_ptr
    stg = {"i": 0, "t": [P.alloc(f"stgB{i}", [128, 2048], F32) for i in range(6)]}
    wK = P.alloc("wK", [128, 8, 1024], BF16)
    wV = P.alloc("wV", [128, 8, 1024], BF16)
    memT = P.alloc("memT", [128, 8, MT], BF16)
    P.dma(prmB2[:], prmB_d[:, 4 * 1024:6 * 1024], "G:B2", w=["prmB2"])
    P.dma(identQ[:], prmA_d[:, C_ID:C_ID + 128], "G:B2", w=["identB"])
    P.dve(lambda e: e.tensor_copy(identQb[:], identQ[:]), r=["identB"], w=["identBb"])
    load_weight(wQ, w_mq_d, 8, 0, 1024, 0, stg, "wQ")
    load_weight(wMO, w_mo_d, 8, 0, 1024, 0, stg, "wMO")
    load_weight(wK, w_mk_d, 8, 0, 1024, 0, stg, "wK")
    load_weight(wV, w_mv_d, 8, 0, 1024, 0, stg, "wV")
    for si in range(NS):
        for mc in range(NM):
            s_ = stg["i"] % 3
            stg["i"] += 1
            sk = f"stg{s_}"
            stt = stg["t"][s_]
            r0 = si * MT + mc * 128
            P.dma(stt[:, 0:1024], mem_d[r0:r0 + 128, :], sk, w=[sk])
            for half in range(2):
                pt, pk = pb()
                for j in range(4):
                    kc = half * 4 + j
                    P.pe(lambda e, kc=kc, j=j, pt=pt, stt=stt: e.transpose(pt[:, j * 128:(j + 1) * 128], stt[:, kc * 128:(kc + 1) * 128], identQ[:]),
                         r=[sk, "identB"], w=pk)
                P.dve(lambda e, pt=pt, half=half, mc=mc: e.tensor_copy(memT[:, half * 4:half * 4 + 4, mc * 128:(mc + 1) * 128],
                                                                      pt.rearrange("p (c n) -> p c n", n=128)), r=pk, w=["memT"])
        for dc in range(8):
            pt, pk = pb()
            for kc in range(8):
                P.pe(lambda e, dc=dc, kc=kc, pt=pt: e.matmul(pt[:, 0:MT], wK[:, kc, dc * 128:(dc + 1) * 128], memT[:, kc, :],
                                                            start=(kc == 0), stop=(kc == 7)), r=["wK", "memT"], w=pk)
            P.act(lambda e, dc=dc, pt=pt, si=si: e.copy(KT_all[si][:, dc, :], pt[:, 0:MT]), r=pk, w=[f"KT{si}"])
        for mc in range(NM):
            for half in range(2):
                pt, pk = pb()
                for kc in range(8):
                    P.pe(lambda e, mc=mc, kc=kc, half=half, pt=pt: e.matmul(pt, memT[:, kc, mc * 128:(mc + 1) * 128],
                                                                           wV[:, kc, half * 512:(half + 1) * 512],
                                                                           start=(kc == 0), stop=(kc == 7)), r=["wV", "memT"], w=pk)
                P.dve(lambda e, mc=mc, half=half, pt=pt, si=si: e.tensor_copy(Vm_all[si][:, mc, half * 512:(half + 1) * 512], pt),
                      r=pk, w=[f"Vm{si}"])
    P.barrier()
    P.sbuf_ptr = markB2
    BB = []
    for k in range(KB):
        BB.append(dict(x1=P.alloc("x1", [128, 1024], F32), x1T=P.alloc("x1T", [128, 8, 128], BF16), qT=P.alloc("qT", [128, 8, 128], BF16),
                       Pm=P.alloc("Pm", [128, 4, 256], BF16), PT=P.alloc("PT", [128, 8, 128], BF16), r2=P.alloc("r2", [128, 1024], F32),
                       st2=P.alloc("st2", [128, 32], F32)))
    G_L2G, G_L2B = prmB2[:, 0:1024], prmB2[:, 1024:2048]
    P.shared_keys = {"wQ", "wMO", "prmB2", "identB", "identBb"} | {f"KT{i}" for i in range(NS)} | {f"Vm{i}" for i in range(NS)}
    pstate["groups"] = [(i, 1) for i in range(8)]
    pstate["gi"] = [0] * 8

    def b2b_tile(tok0, si, k):
        B = BB[k]
        x1, x1T, qT, Pm, PT, r2, st2 = (B[n] for n in ("x1", "x1T", "qT", "Pm", "PT", "r2", "st2"))
        oT = qT
        P.dma(x1[:], x1sc[tok0:tok0 + 128, :], "x1", w=["x1"])
        yield
        transpose_f32(x1, "x1", x1T, "x1T", identQ[:], "identB")
        yield
        for qh in range(2):
            pqq, pqqk = pb()
            for d4 in range(4):
                dc = qh * 4 + d4
                for kc in range(8):
                    P.pe(lambda e, dc=dc, d4=d4, kc=kc, pqq=pqq: e.matmul(pqq[:, d4 * 128:(d4 + 1) * 128], wQ[:, kc, dc * 128:(dc + 1) * 128],
                                                                         x1T[:, kc, :], start=(kc == 0), stop=(kc == 7)),
                         r=["wQ", "x1Th0", "x1Th1"], w=pqqk)
                if d4 % 2 == 1:
                    yield
            P.act(lambda e, qh=qh, pqq=pqq: e.mul(qT[:, qh * 4:(qh + 1) * 4, :].rearrange("p c n -> p (c n)"), pqq, 256.0 ** -0.5),
                  r=pqqk, w=[f"qT{qh}"])
            yield
        P.pool(lambda e: e.memset(st2[:, 8:12], 0.0), w=["st2h0", "st2h1"])
        for hp in range(2):
            pS, pSk = pb()
            for hh in range(2):
                h = hp * 2 + hh
                for j in range(2):
                    P.pe(lambda e, h=h, hh=hh, j=j, pS=pS: e.matmul(pS[:, hh * 256:(hh + 1) * 256], qT[:, 2 * h + j, :], KT_all[si][:, 2 * h + j, :],
                                                                   start=(j == 0), stop=(j == 1)), r=[f"qT{hp}", f"KT{si}"], w=pSk)
            P.dve(lambda e, hp=hp, pS=pS: e.tensor_reduce(out=st2[:, 2 * hp:2 * hp + 2], in_=pS.rearrange("p (h m) -> p h m", h=2), axis=AX.X, op=ALU.max),
                  r=pSk, w=[f"st2f{hp}"])
            P.dve(lambda e, hp=hp: e.tensor_scalar(out=st2[:, 4 + 2 * hp:6 + 2 * hp], in0=st2[:, 2 * hp:2 * hp + 2], scalar1=-1.0, scalar2=None, op0=ALU.mult),
                  r=[f"st2f{hp}"], w=[f"st2g{hp}"])
            yield
            for hh in range(2):
                h = hp * 2 + hh
                P.act(lambda e, h=h, hh=hh, pS=pS: e.activation(out=Pm[:, h, :], in_=pS[:, hh * 256:(hh + 1) * 256], func=AF.Exp, bias=st2[:, 4 + h:5 + h],
                                                               accum_out=st2[:, 8 + h:9 + h]), r=pSk + [f"st2g{hp}"], w=[f"Pm{hp}", f"st2h{hp}"])
            yield
        P.dve(lambda e: e.reciprocal(st2[:, 12:16], st2[:, 8:12]), r=["st2h0", "st2h1"], w=["st2i"])
        P.dve(lambda e: e.tensor_tensor(out=Pm[:], in0=Pm[:], in1=st2[:, 12:16].unsqueeze(2).to_broadcast([128, 4, 256]), op=ALU.mult),
              r=["Pm0", "Pm1", "st2i"], w=["Pm0", "Pm1"])
        yield
        pt, pk = pb()
        ptb = pt.bitcast(BF16)
        for h in range(4):
            for j in range(2):
                c = h * 2 + j
                P.pe(lambda e, c=c, h=h, j=j, ptb=ptb: e.transpose(ptb[:, c * 128:(c + 1) * 128], Pm[:, h, j * 128:(j + 1) * 128], identQb[:]),
                     r=["Pm0", "Pm1", "identBb"], w=pk)
        P.act(lambda e, ptb=ptb: e.copy(PT[:].rearrange("p c n -> p (c n)"), ptb), r=pk, w=["PT"])
        yield
        for oh in range(2):
            poT, poTk = pb()
            for c4 in range(4):
                c = oh * 4 + c4
                h, j2 = c // 2, c % 2
                for j in range(2):
                    P.pe(lambda e, c4=c4, h=h, j=j, j2=j2, poT=poT: e.matmul(poT[:, c4 * 128:(c4 + 1) * 128],
                                                                            Vm_all[si][:, j, h * 256 + j2 * 128:h * 256 + (j2 + 1) * 128], PT[:, h * 2 + j, :],
                                                                            start=(j == 0), stop=(j == 1)), r=[f"Vm{si}", "PT"], w=poTk)
            P.act(lambda e, oh=oh, poT=poT: e.copy(oT[:, oh * 4:(oh + 1) * 4, :].rearrange("p c n -> p (c n)"), poT), r=poTk, w=["qT0", "qT1", f"oT{oh}"])
            yield
        for half in range(2):
            po2, po2k = pb()
            for c in range(8):
                P.pe(lambda e, c=c, half=half, po2=po2: e.matmul(po2, oT[:, c, :], wMO[:, c, half * 512:(half + 1) * 512],
                                                                start=(c == 0), stop=(c == 7)), r=["oT0", "oT1", "qT0", "qT1", "wMO"], w=po2k)
            P.dve(lambda e, half=half, po2=po2: e.scalar_tensor_tensor(out=r2[:, half * 512:(half + 1) * 512], in0=x1[:, half * 512:(half + 1) * 512],
                                                                      scalar=ALPHA, in1=po2, op0=ALU.mult, op1=ALU.add),
                  r=["x1"] + po2k, w=["r2"])
            yield
        layer_norm(r2[:], "r2", G_L2G, G_L2B, "prmB2", x1[:], "x1", st2[:, 16:32], "st2j")
        P.dma(x2sc2[tok0:tok0 + 128, :], x1[:], "x1s", r=["x1"])

    run_streams(tiles, lambda it, k: b2b_tile(it[0], it[1], k), KB, 4)
    P.shared_keys = set()
    x2sc = x2sc2

    P.barrier()
    P.sbuf_ptr = base_ptr
    NB = 2
    w1 = P.alloc("w1", [128, 8, 4096], BF16)
    w2 = P.alloc("w2", [128, 32, 1024], BF16)
    prmC = P.alloc("prmC", [128, 2048], F32)
    identC = P.alloc("identC", [128, 128], F32)
    markC = P.sbuf_ptr
    stg = {"i": 0, "t": [P.alloc(f"stgC{i}", [128, 2048], F32) for i in range(6)]}
    P.dma(prmC[:], prmB_d[:, 6 * 1024:8 * 1024], "G:C", w=["prmC"])
    P.dma(identC[:], prmA_d[:, C_ID:C_ID + 128], "G:C", w=["identC"])
    load_weight(w1, w_ff1_d, 8, 0, 4096, 0, stg, "w1")
    load_weight(w2, w_ff2_d, 32, 0, 1024, 0, stg, "w2")
    P.barrier()
    P.sbuf_ptr = markC
    x2b = [P.alloc(f"x2b{i}", [128, NB, 1024], F32) for i in range(2)]
    x2T = P.alloc("x2T", [128, 8, NB * 128], BF16)
    h1T = P.alloc("h1T", [128, 32, NB * 128], BF16)
    rtmp = [P.alloc(f"rtmp{i}", [128, 2, NB * 128], BF16) for i in range(2)]
    r3 = P.alloc("r3", [128, 1024], F32)
    yo3 = [P.alloc(f"yo3{i}", [128, 1024], F32) for i in range(2)]
    st3 = P.alloc("st3", [128, 16], F32)

    blocks = []
    for si in range(NS):
        nt = seq_lens[si] // 128
        for b in range(0, nt, NB):
            blocks.append((seq_offs[si] + b * 128, min(NB, nt - b)))

    def b3_load(tok0, nb, slot):
        P.dma(x2b[slot][:, 0:nb, :], x2sc[tok0:tok0 + nb * 128, :].rearrange("(t p) d -> p t d", p=128), f"x2b{slot}", w=[f"x2b{slot}"])

    octr = {"i": 0}
    b3_load(blocks[0][0], blocks[0][1], 0)
    for j, (tok0, nb) in enumerate(blocks):
        slot = j % 2
        if j + 1 < len(blocks):
            b3_load(blocks[j + 1][0], blocks[j + 1][1], (j + 1) % 2)
        xk = f"x2b{slot}"
        N = nb * 128
        for t in range(nb):
            transpose_f32(x2b[slot][:, t, :], xk, x2T, "x2T", identC[:], "identC", c0=t * 128)
        for fp in range(16):
            pt, pk = pb()
            for q in range(2):
                fc = fp * 2 + q
                for kc in range(8):
                    P.pe(lambda e, fc=fc, kc=kc, q=q, pt=pt, N=N: e.matmul(pt[:, q * N:(q + 1) * N], w1[:, kc, fc * 128:(fc + 1) * 128], x2T[:, kc, 0:N],
                                                                          start=(kc == 0), stop=(kc == 7)), r=["w1", "x2Th0", "x2Th1"], w=pk)
            rt = rtmp[fp % 2]
            rk = f"rtmp{fp % 2}"
            P.act(lambda e, rt=rt, pt=pt, N=N: e.activation(out=rt[:, :, 0:N], in_=pt[:, 0:2 * N].rearrange("p (q n) -> p q n", q=2), func=AF.Relu),
                  r=pk, w=[rk])
            P.pool(lambda e, rt=rt, fp=fp, N=N: e.tensor_tensor(out=h1T[:, fp * 2:fp * 2 + 2, 0:N], in0=rt[:, :, 0:N], in1=rt[:, :, 0:N], op=ALU.mult),
                   r=[rk], w=[f"h1T{fp // 4}"])
        for t in range(nb):
            p3, p3k = pb(2)
            for half in range(2):
                for fc in range(32):
                    P.pe(lambda e, fc=fc, half=half, t=t, p3=p3: e.matmul(p3[:, half * 512:(half + 1) * 512], h1T[:, fc, t * 128:(t + 1) * 128],
                                                                         w2[:, fc, half * 512:(half + 1) * 512], start=(fc == 0), stop=(fc == 31)),
                         r=[f"h1T{fc // 8}", "w2"], w=[p3k[half]])
            P.dve(lambda e, t=t, p3=p3, slot=slot: e.scalar_tensor_tensor(out=r3[:], in0=x2b[slot][:, t, :], scalar=ALPHA, in1=p3, op0=ALU.mult, op1=ALU.add),
                  r=[xk] + p3k, w=["r3"])
            os_ = octr["i"] % 2
            octr["i"] += 1
            layer_norm(r3[:], "r3", prmC[:, 0:1024], prmC[:, 1024:2048], "prmC", yo3[os_][:], f"yo3{os_}", st3[:], "st3")
            P.dma(y_d[tok0 + t * 128:tok0 + (t + 1) * 128, :], yo3[os_][:], f"yo3{os_}s", r=[f"yo3{os_}"])

    finals = [k for k in P.dma_last if k.startswith("yo3") and k.endswith("s")]
    P.emit(final_wait_keys=finals)
    st.close()
    return nc, P


_CACHE = {}


def _prep_shared(inp):
    g = lambda k: np.asarray(inp[k], dtype=np.float32)[0]
    prmA = np.zeros((128, NPA), np.float32)
    prmA[:, 0:1280] = host_consts()
    cw = g("conv_w")
    prmA[:, C_CW:C_CW + 60] = cw.reshape(5, 12, 128).transpose(2, 1, 0).reshape(128, 60)
    prmA[:, C_CB:C_CB + 12] = g("conv_b").reshape(12, 128).T
    prmA[:, C_DTB:C_DTB + 16] = g("dt_bias_f")[None, :]
    prmA[:, C_DTB + 16:C_DTB + 32] = g("dt_bias_b")[None, :]
    prmA[:, C_AL:C_AL + 16] = g("a_log_f")[None, :]
    prmA[:, C_AL + 16:C_AL + 32] = g("a_log_b")[None, :]
    prmA[:, C_DS:C_DS + 16] = g("d_skip")[None, :]
    rowp = np.zeros((17, 2560), np.float32)
    rowp[0:16, 0:512] = g("w_gk_f")
    rowp[16, 0:512] = g("b_gk_f")
    rowp[0:16, 512:1024] = g("w_gk_b")
    rowp[16, 512:1024] = g("b_gk_b")
    rowp[0, 1024:2560] = g("conv_b")
    prmB = np.zeros((128, 8 * 1024), np.float32)
    prmB[:, 0:1024] = g("ssd_norm_g")[None, :]
    prmB[:, 1024:2048] = np.tile(g("gla_norm_g"), 4)[None, :]
    for i, k in enumerate(["ln1_g", "ln1_b", "ln2_g", "ln2_b", "ln3_g", "ln3_b"]):
        prmB[:, (2 + i) * 1024:(3 + i) * 1024] = g(k)[None, :]
    shared = {"prmA": prmA, "rowp": rowp, "prmB": prmB}
    for k in ["w_in", "w_out", "w_mq", "w_mk", "w_mv", "w_mo", "w_ff1", "w_ff2"]:
        shared[k] = np.ascontiguousarray(g(k))
    return shared


def kernel(**inp):
    xp = np.asarray(inp["x_prompt"], dtype=np.float32)
    xs = np.asarray(inp["x_sample"], dtype=np.float32)
    mp = np.asarray(inp["mem_prompt"], dtype=np.float32)
    ms = np.asarray(inp["mem_sample"], dtype=np.float32)
    n = 8
    Lp, Ls = xp.shape[1], xs.shape[1]
    key = (Lp, Ls)
    if key not in _CACHE:
        _CACHE[key] = build([Lp, Ls], mem_tokens=mp.shape[1])[0]
    nc = _CACHE[key]
    shared = _prep_shared(inp)
    in_maps = []
    for c in range(n):
        m = dict(shared)
        m["x"] = np.concatenate([xp[c], xs[c]], axis=0)
        m["mem"] = np.concatenate([mp[c], ms[c]], axis=0)
        in_maps.append(m)
    res = run_bass_kernel_spmd(nc, in_maps, core_ids=list(range(n)))
    yp = np.stack([res.results[c]["y"][:Lp] for c in range(n)], axis=0)
    ys = np.stack([res.results[c]["y"][Lp:] for c in range(n)], axis=0)
    return (yp.astype(np.float32), ys.astype(np.float32))
```
